# Optimizing a Trainium2 kernel written in Bass

```python
import jax, jax.numpy as jnp
from jax import lax
import numpy as np

D_MODEL = 2048
BATCH = 4
SEQ = 8192
DEPTH = 1
DEC_BATCH = 16
DEC_SEQ = 32
PAST_LEN = 1024

CHUNK = 64
HEAD_DIM = 128
FOX_WIDTH = D_MODEL // 2
N_FOX_HEADS = FOX_WIDTH // HEAD_DIM
POOL_WIDTH = D_MODEL - FOX_WIDTH
POOL_WINDOWS = (2, 4, 8, 16)
N_POOL_GROUPS = len(POOL_WINDOWS)
POOL_GROUP_DIM = POOL_WIDTH // N_POOL_GROUPS
POOL_HIST = max(POOL_WINDOWS) - 1
MIX_WIDTH = FOX_WIDTH + POOL_WIDTH
IN_WIDTH = 3 * FOX_WIDTH + N_FOX_HEADS + POOL_WIDTH
D_FF = 256 * ((8 * D_MODEL + 3 * 256 - 1) // (3 * 256))
Q_BLOCK = 128
EPS = 1e-6
ATTN_SCALE = HEAD_DIM ** -0.5

kernel_name = 'fox_pool_macaron_stream_step'


def _rms_norm(x, g):
    xf = x.astype(jnp.float32)
    y = xf * lax.rsqrt(jnp.mean(xf * xf, axis=-1, keepdims=True) + EPS)
    return (y * g.astype(jnp.float32)).astype(x.dtype)


def _swiglu(x, w_gate, w_up, w_down):
    return (jax.nn.silu(x @ w_gate) * (x @ w_up)) @ w_down


def _split_groups(u, b_f):
    B, T, _ = u.shape
    hs = (B, T, N_FOX_HEADS, HEAD_DIM)
    q = u[..., :FOX_WIDTH].reshape(hs)
    k = u[..., FOX_WIDTH:2 * FOX_WIDTH].reshape(hs)
    v = u[..., 2 * FOX_WIDTH:3 * FOX_WIDTH].reshape(hs)
    o = 3 * FOX_WIDTH
    logf = jax.nn.log_sigmoid(u[..., o:o + N_FOX_HEADS].astype(jnp.float32) + b_f.astype(jnp.float32))
    p = u[..., o + N_FOX_HEADS:]
    return q, k, v, logf, p


def _fox_block(q, k, v, c_q, c_k, q_pos, k_pos):
    s = jnp.einsum('bqhd,bkhd->bhqk', q, k).astype(jnp.float32) * ATTN_SCALE
    s = s + c_q[..., :, None] - c_k[:, :, None, :]
    s = jnp.where(k_pos[None, :] <= q_pos[:, None], s, -jnp.inf)
    p = jax.nn.softmax(s, axis=-1).astype(v.dtype)
    return jnp.einsum('bhqk,bkhd->bqhd', p, v)


def _fox_prompt(q, k, v, logf):
    B, S, H, Dh = q.shape
    nb = S // Q_BLOCK
    c = jnp.cumsum(logf, axis=1).transpose(0, 2, 1)
    qb = q.reshape(B, nb, Q_BLOCK, H, Dh).swapaxes(0, 1)
    cqb = c.reshape(B, H, nb, Q_BLOCK).transpose(2, 0, 1, 3)
    pos = jnp.arange(S, dtype=jnp.int32)
    qposb = pos.reshape(nb, Q_BLOCK)
    out = lax.map(lambda blk: _fox_block(blk[0], k, v, blk[1], c, blk[2], pos), (qb, cqb, qposb))
    return out.swapaxes(0, 1).reshape(B, S, H * Dh)


def _fox_sample(q, k, v, logf, cache_k, cache_v, cache_logf):
    B, T, H, Dh = q.shape
    P = cache_k.shape[1]
    k_all = jnp.concatenate([cache_k.astype(k.dtype), k], axis=1)
    v_all = jnp.concatenate([cache_v.astype(v.dtype), v], axis=1)
    lf_all = jnp.concatenate([cache_logf.astype(jnp.float32), logf], axis=1)
    c = jnp.cumsum(lf_all, axis=1).transpose(0, 2, 1)
    k_pos = jnp.arange(P + T, dtype=jnp.int32)
    q_pos = P + jnp.arange(T, dtype=jnp.int32)
    out = _fox_block(q, k_all, v_all, c[:, :, P:], c, q_pos, k_pos)
    return out.reshape(B, T, H * Dh)


def _pool_mix(ext, pos_out, w_pool, pool_scale):
    L = ext.shape[1]
    n = pos_out.shape[0]
    start = L - n
    xf = ext.astype(jnp.float32)
    cs = jnp.cumsum(xf, axis=1)
    cs0 = jnp.concatenate([jnp.zeros_like(cs[:, :1]), cs], axis=1)
    outs = []
    for g, w in enumerate(POOL_WINDOWS):
        sl = slice(g * POOL_GROUP_DIM, (g + 1) * POOL_GROUP_DIM)
        tot = cs0[:, start + 1:L + 1, sl] - cs0[:, start + 1 - w:L + 1 - w, sl]
        cnt = jnp.minimum(pos_out + 1, w).astype(jnp.float32)[None, :, None]
        d = (tot / cnt - xf[:, start:, sl]).astype(ext.dtype)
        outs.append(d @ w_pool[g])
    return jnp.concatenate(outs, axis=-1) * pool_scale


def _pre_mix(x, g_ffn1, w1_gate, w1_up, w1_down, g_mix, w_in, b_f):
    h = x + 0.5 * _swiglu(_rms_norm(x, g_ffn1), w1_gate, w1_up, w1_down)
    u = _rms_norm(h, g_mix) @ w_in
    q, k, v, logf, p = _split_groups(u, b_f)
    return h, q, k, v, logf, p


def _post_mix(h, fox_out, pool_out, w_o, g_ffn2, w2_gate, w2_up, w2_down):
    h = h + jnp.concatenate([fox_out, pool_out.astype(fox_out.dtype)], axis=-1) @ w_o
    return h + 0.5 * _swiglu(_rms_norm(h, g_ffn2), w2_gate, w2_up, w2_down)


def setup_inputs(seed: int = 0) -> dict:
    key = jax.random.key(seed)
    ks = jax.random.split(key, 24)

    def nrm(k, shape, scale=1.0):
        return jax.random.normal(k, shape, jnp.float32) * scale

    Ld = DEPTH
    inp = {}
    inp['x_prompt'] = nrm(ks[0], (BATCH, SEQ, D_MODEL))
    inp['x_sample'] = nrm(ks[1], (DEC_BATCH, DEC_SEQ, D_MODEL))
    inp['cache_k'] = nrm(ks[2], (Ld, DEC_BATCH, PAST_LEN, N_FOX_HEADS, HEAD_DIM))
    inp['cache_v'] = nrm(ks[3], (Ld, DEC_BATCH, PAST_LEN, N_FOX_HEADS, HEAD_DIM))
    inp['cache_logf'] = jax.nn.log_sigmoid(
        jax.random.uniform(ks[4], (Ld, DEC_BATCH, PAST_LEN, N_FOX_HEADS), jnp.float32, 1.0, 5.0)
        + nrm(ks[5], (Ld, DEC_BATCH, PAST_LEN, N_FOX_HEADS), 0.5))
    inp['state_pool'] = nrm(ks[6], (Ld, DEC_BATCH, POOL_HIST, POOL_WIDTH))
    inp['g_ffn1'] = 1.0 + nrm(ks[7], (Ld, D_MODEL), 0.02)
    inp['w1_gate'] = nrm(ks[8], (Ld, D_MODEL, D_FF), D_MODEL ** -0.5)
    inp['w1_up'] = nrm(ks[9], (Ld, D_MODEL, D_FF), D_MODEL ** -0.5)
    inp['w1_down'] = nrm(ks[10], (Ld, D_FF, D_MODEL), D_FF ** -0.5)
    inp['g_mix'] = 1.0 + nrm(ks[11], (Ld, D_MODEL), 0.02)
    inp['w_in'] = nrm(ks[12], (Ld, D_MODEL, IN_WIDTH), D_MODEL ** -0.5)
    inp['b_f'] = jax.random.uniform(ks[13], (Ld, N_FOX_HEADS), jnp.float32, 1.0, 5.0)
    inp['w_pool'] = nrm(ks[14], (Ld, N_POOL_GROUPS, POOL_GROUP_DIM, POOL_GROUP_DIM), POOL_GROUP_DIM ** -0.5)
    inp['pool_scale'] = 1.0 + nrm(ks[15], (Ld, POOL_WIDTH), 0.1)
    inp['w_o'] = nrm(ks[16], (Ld, MIX_WIDTH, D_MODEL), MIX_WIDTH ** -0.5)
    inp['g_ffn2'] = 1.0 + nrm(ks[17], (Ld, D_MODEL), 0.02)
    inp['w2_gate'] = nrm(ks[18], (Ld, D_MODEL, D_FF), D_MODEL ** -0.5)
    inp['w2_up'] = nrm(ks[19], (Ld, D_MODEL, D_FF), D_MODEL ** -0.5)
    inp['w2_down'] = nrm(ks[20], (Ld, D_FF, D_MODEL), D_FF ** -0.5)
    inp['g_final'] = 1.0 + nrm(ks[21], (D_MODEL,), 0.02)
    return inp


def reference(x_prompt, x_sample, cache_k, cache_v, cache_logf, state_pool,
              g_ffn1, w1_gate, w1_up, w1_down, g_mix, w_in, b_f, w_pool, pool_scale,
              w_o, g_ffn2, w2_gate, w2_up, w2_down, g_final):
    B, S, _ = x_prompt.shape
    Tn = x_sample.shape[1]
    P = cache_k.shape[2]
    pos_prompt = jnp.arange(S, dtype=jnp.int32)
    pos_sample = P + jnp.arange(Tn, dtype=jnp.int32)

    hp, hs = x_prompt, x_sample
    kp_l, vp_l, lfp_l, pp_l = [], [], [], []
    ks_l, vs_l, lfs_l, ps_l = [], [], [], []
    for l in range(DEPTH):
        h, q, k, v, logf, p = _pre_mix(hp, g_ffn1[l], w1_gate[l], w1_up[l], w1_down[l], g_mix[l], w_in[l], b_f[l])
        fox_out = _fox_prompt(q, k, v, logf)
        ext = jnp.concatenate([jnp.zeros((B, POOL_HIST, POOL_WIDTH), p.dtype), p], axis=1)
        pool_out = _pool_mix(ext, pos_prompt, w_pool[l], pool_scale[l])
        hp = _post_mix(h, fox_out, pool_out, w_o[l], g_ffn2[l], w2_gate[l], w2_up[l], w2_down[l])
        kp_l.append(k); vp_l.append(v); lfp_l.append(logf); pp_l.append(ext[:, -POOL_HIST:])

        h, q, k, v, logf, p = _pre_mix(hs, g_ffn1[l], w1_gate[l], w1_up[l], w1_down[l], g_mix[l], w_in[l], b_f[l])
        fox_out = _fox_sample(q, k, v, logf, cache_k[l], cache_v[l], cache_logf[l])
        ext = jnp.concatenate([state_pool[l].astype(p.dtype), p], axis=1)
        pool_out = _pool_mix(ext, pos_sample, w_pool[l], pool_scale[l])
        hs = _post_mix(h, fox_out, pool_out, w_o[l], g_ffn2[l], w2_gate[l], w2_up[l], w2_down[l])
        ks_l.append(k); vs_l.append(v); lfs_l.append(logf); ps_l.append(ext[:, -POOL_HIST:])

    y_prompt = _rms_norm(hp, g_final)
    y_sample = _rms_norm(hs, g_final)
    k_prompt = jnp.stack(kp_l)
    v_prompt = jnp.stack(vp_l)
    logf_prompt = jnp.stack(lfp_l)
    pool_prompt = jnp.stack(pp_l)
    k_sample = jnp.stack(ks_l)
    v_sample = jnp.stack(vs_l)
    logf_sample = jnp.stack(lfs_l)
    pool_sample = jnp.stack(ps_l)
    return (y_prompt, y_sample, k_prompt, v_prompt, logf_prompt, pool_prompt, k_sample, v_sample, logf_sample, pool_sample)
```

```python
import numpy as np
import ml_dtypes
import concourse.bass as bass
import concourse.mybir as mybir
from concourse.bass_utils import run_bass_kernel_spmd

F32 = mybir.dt.float32
BF16 = mybir.dt.bfloat16
AF = mybir.ActivationFunctionType
ALU = mybir.AluOpType

D = 2048
DFF = 5632
NFC = DFF // 128
KC = D // 128
T = 1024
NB = T // 128
H = 8
HD = 128
INW = 4104
EPS = 1e-6
SCALE = HD ** -0.5
NEG = -30000.0
SLOT_B = 8192
NRING = 5
PREFETCH = 3


class Buf:
    __slots__ = ("name", "w", "r")

    def __init__(self, name):
        self.name = name
        self.w = {}
        self.r = {}


class Eng:
    def __init__(self, nc, name, eng, inorder_safe=False):
        self.name = name
        self.eng = eng
        self.sem = nc.alloc_semaphore("sem_" + name)
        self.count = 0
        self.waited = {}
        self.inorder_safe = inorder_safe


class Trk:
    def __init__(self, nc):
        self.nc = nc
        self.pe = Eng(nc, "pe", nc.tensor, True)
        self.act = Eng(nc, "act", nc.scalar)
        self.dve = Eng(nc, "dve", nc.vector)
        self.pool = Eng(nc, "pool", nc.gpsimd)
        self.sp = Eng(nc, "sp", nc.sync, True)
        self.dma_sems = {}

    def _wait(self, E, evs, self_ok=False):
        for key, (sem, val) in evs.items():
            if sem is E.sem and (E.inorder_safe or self_ok):
                continue
            if E.waited.get(key, 0) >= val:
                continue
            E.eng.wait_ge(sem, val)
            E.waited[key] = val

    @staticmethod
    def _deps(reads, writes):
        evs = {}

        def add(d):
            for k, (s, v) in d.items():
                if k not in evs or evs[k][1] < v:
                    evs[k] = (s, v)
        for b in reads:
            add(b.w)
        for b in writes:
            add(b.w)
            add(b.r)
        return evs

    @staticmethod
    def _record(key, ev, reads, writes):
        for b in reads:
            if key not in b.r or b.r[key][1] < ev[1]:
                b.r[key] = ev
        for b in writes:
            b.w = {key: ev}
            b.r = {}

    def op(self, E, fn, reads=(), writes=(), signal=True, self_ok=False):
        self._wait(E, self._deps(reads, writes), self_ok)
        inst = fn()
        if signal:
            E.count += 1
            inst.then_inc(E.sem, 1)
            ev = (E.sem, E.count)
        else:
            ev = (E.sem, E.count + 1)
        self._record(E.name, ev, reads, writes)
        return inst

    def dma(self, Q, out, in_, reads=(), writes=(), dbuf=None, slow=False):
        if dbuf is None:
            dbuf = writes[0] if writes else reads[0]
        key = "dma_" + dbuf.name
        deps = self._deps(reads, writes)
        deps.pop(key, None)
        self._wait(Q, deps)
        if key not in self.dma_sems:
            self.dma_sems[key] = [self.nc.alloc_semaphore(key), 0]
        ent = self.dma_sems[key]
        ent[1] += 16
        if slow:
            with self.nc.allow_non_contiguous_dma(reason="small strided transfer"):
                Q.eng.dma_start(out=out, in_=in_).then_inc(ent[0], 16)
        else:
            Q.eng.dma_start(out=out, in_=in_).then_inc(ent[0], 16)
        self._record(key, (ent[0], ent[1]), reads, writes)

    @staticmethod
    def handoff(old, new):
        evs = {}
        for b in old:
            for d in (b.w, b.r):
                for k, (s, v) in d.items():
                    if k not in evs or evs[k][1] < v:
                        evs[k] = (s, v)
        for b in new:
            b.w = {}
            b.r = dict(evs)

    def finish(self, E):
        for key, (sem, val) in self.dma_sems.items():
            if E.waited.get(key, 0) < val:
                E.eng.wait_ge(sem, val)
                E.waited[key] = val


class Prog:
    def __init__(self, npair, with_sample):
        self.npair = npair
        self.nslot = 2 * npair
        self.with_sample = with_sample
        nc = self.nc = bass.Bass("TRN2", target_bir_lowering=False)
        self.tk = Trk(nc)
        self._dram()
        self._sbuf()
        self.loads = []
        self.issued = 0
        self.next_load = 0

    def _dram(self):
        nc = self.nc
        ns = self.nslot
        di = lambda n, s, dt=F32: nc.dram_tensor(n, list(s), dt, kind="ExternalInput").ap()
        do = lambda n, s, dt=F32: nc.dram_tensor(n, list(s), dt, kind="ExternalOutput").ap()
        dn = lambda n, s, dt=BF16: nc.dram_tensor(n, list(s), dt).ap()
        self.xs = di("xs", (ns * T, D))
        self.w_gate = [di("w1_gate", (D, DFF)), di("w2_gate", (D, DFF))]
        self.w_up = [di("w1_up", (D, DFF)), di("w2_up", (D, DFF))]
        self.w_down = [di("w1_down", (DFF, D)), di("w2_down", (DFF, D))]
        self.w_in = di("w_in", (D, INW))
        self.w_o = di("w_o", (D, D))
        self.wf_d = di("w_f", (128, KC * 8))
        self.w_pool = di("w_pool", (4 * 256, 256))
        self.gcols = di("gcols", (128, 3 * KC))
        self.gfin = di("gfin", (128, D))
        self.bfcol = di("bfcol", (8, 1))
        self.pscale = di("pscale", (128, 8))
        self.identb_d = di("identb", (128, 128), BF16)
        self.identf_d = di("identf", (128, 128))
        self.trib_d = di("trib", (128, 128), BF16)
        self.onesf_d = di("onesf", (128, 128))
        self.kmask_d = di("kmask", (128, 64))
        self.diag8_d = di("diag8", (8, 64))
        self.selb_d = di("selb", (16, 1024), BF16)
        self.poolfix_d = di("poolfix", (128, 64))
        self.xsmp = di("xsmp", (64, D))
        self.ckT = di("ckT", (2 * H * 128, 1024))
        self.cv = di("cv", (2 * H * 128, 1024))
        self.clfT = di("clfT", (2 * H, 1024))
        self.spT = di("spT", (128, 256))
        self.mask64_d = di("mask64", (64, 64), BF16)
        self.ys_out = do("ys_out", (64, D))
        self.ks_out = do("ks_out", (64, H * HD))
        self.vs_out = do("vs_out", (64, H * HD))
        self.lfs_out = do("lfs_out", (64, H))
        self.ps_out = do("ps_out", (30, 1024))
        self.y_out = do("y_out", (self.npair * T, D))
        self.k_out = do("k_out", (ns * T, H * HD))
        self.v_out = do("v_out", (ns * T, H * HD))
        self.lf_out = do("lf_out", (ns * 128, NB * H))
        self.pool_out = do("pool_out", (15, 1024))
        self.wg_s = [dn("wg1_s", (D, DFF)), dn("wg2_s", (D, DFF))]
        self.wu_s = [dn("wu1_s", (D, DFF)), dn("wu2_s", (D, DFF))]
        self.wd_s = [dn("wd1_s", (DFF, D)), dn("wd2_s", (DFF, D))]
        self.win_s = dn("win_s", (D, INW))
        self.wo_s = dn("wo_s", (D, D))
        self.wp_s = dn("wp_s", (4 * 256, 256))
        self.kt_hist = dn("kt_hist", (ns * H * 128, T))
        self.v_hist = dn("v_hist", (ns * H * 128, NB * HD))
        B = Buf
        self.b_wg = [[B("wg%d_%d" % (f, q)) for q in range(4)] for f in range(2)]
        self.b_wu = [[B("wu%d_%d" % (f, q)) for q in range(4)] for f in range(2)]
        self.b_wd = [[B("wd%d_%d" % (f, q)) for q in range(4)] for f in range(2)]
        self.b_win = B("win")
        self.b_wo = B("wo")
        self.b_wp = B("wp")
        self.b_kth = [[B("kth%d_%d" % (s, h)) for h in range(H)] for s in range(ns)]
        self.b_vh = [[B("vh%d_%d" % (s, h)) for h in range(H)] for s in range(ns)]
        self.b_out = B("outs")

    def _sbuf(self):
        nc = self.nc
        sb = lambda n, cols, dt: nc.alloc_sbuf_tensor(n, [128, cols], dt)
        self.R = sb("R", NB * D, F32)
        self.bR = [[Buf("R%d_%d" % (b, q)) for q in range(4)] for b in range(NB)]
        self.bRld = [Buf("Rld%d" % b) for b in range(NB)]
        self.bRst = [Buf("Rst%d" % b) for b in range(NB)]
        self.XT = sb("XT", KC * T, BF16)
        self.bXT = Buf("XT")
        self.HM = sb("HM", 2 * 4 * T, BF16)
        self.bHM = [Buf("HM0"), Buf("HM1")]
        self.RING = sb("RING", NRING * SLOT_B // 2, BF16)
        self.bRING = [Buf("ring%d" % i) for i in range(NRING)]
        self.QT = sb("QT", H * T, BF16)
        self.bQT = [Buf("QT%d" % h) for h in range(H)]
        self.TMP = sb("TMP", 12288 // 4, F32)
        self.GFIN = sb("GFIN", D, F32)
        self.PT = sb("PT", 3 * 512, BF16)
        self.bPT = [Buf("PT%d" % i) for i in range(3)]
        self.ONESB = sb("ONESB", 128, BF16)
        self.HS = sb("HS", T, BF16)
        self.bHS = Buf("HS")
        self.SELB = sb("SELB", 1024, BF16)
        self.BTK = sb("BTK", 128, F32)
        self.bBTK = [Buf("BTK0"), Buf("BTK1")]
        self.CK = sb("CK", H * 64, F32)
        self.bCK = Buf("CK")
        self.CR = sb("CR", H * NB, F32)
        self.bCR = Buf("CR")
        self.BT = sb("BT", 4 * NB, F32)
        self.bBT = [Buf("BT%d" % i) for i in range(4)]
        self.CONST = sb("CONST", 704, F32)
        self.bC = Buf("const")
        c = self.CONST
        self.identf = c[:, 0:128]
        self.onesf = c[:, 128:256]
        self.identb = c[:, 256:320].bitcast(BF16)
        self.trib = c[:, 320:384].bitcast(BF16)
        self.pad0 = c[:, 384:448]
        self.gc = c[:, 448:496]
        self.psc = c[:, 496:504]
        self.kmask = c[:, 504:568]
        self.poolfix = c[:, 568:632]
        self.diag8 = c[0:8, 632:696]
        self.bfc = c[0:8, 696:697]
        self.nbfc = c[0:8, 697:698]
        self.cprev = c[0:8, 698:699]
        self.onesrow = c[0:8, 700:704]
        self.SMP = sb("SMP", 512, F32)
        self.SPT = sb("SPT", 256, F32)
        self.M64 = sb("M64", 64, BF16)
        self.ST = sb("ST", 32, F32)
        self.bST = Buf("ST")
        self.HALO = sb("HALO", 8 * 16, F32)
        self.bHALO = Buf("HALO")
        self.PS = [nc.alloc_psum_tensor("ps%d" % i, [128, 512], F32) for i in range(7)]
        self.PSB = nc.alloc_psum_tensor("psb", [128, 1024], BF16)
        self.bPS = [Buf("ps%d" % i) for i in range(8)]

    def tmp_take(self, new):
        Trk.handoff(self.tmp_cur, new)
        self.tmp_cur = list(new)

    def hm_take(self, new):
        if new is self.hm_cur:
            return
        Trk.handoff(self.hm_cur, new)
        self.hm_cur = new

    def qt_take(self, new):
        if new is self.qt_cur:
            return
        Trk.handoff(self.qt_cur, new)
        self.qt_cur = new

    def slot_ap(self, i):
        return self.RING[:, i * (SLOT_B // 2):(i + 1) * (SLOT_B // 2)]

    def add_load(self, fn):
        self.loads.append(fn)
        return len(self.loads) - 1

    def acquire(self, j):
        lim = min(len(self.loads), j + 1 + PREFETCH)
        while self.issued < lim:
            i = self.issued
            s = i % NRING
            self.loads[i](self.slot_ap(s), self.bRING[s])
            self.issued += 1
        s = j % NRING
        return self.slot_ap(s), self.bRING[s]

    def prologue(self):
        nc, tk = self.nc, self.tk
        c = self.CONST
        cd = lambda dst, src: tk.dma(tk.sp, dst, src, writes=[self.bC])
        cd(self.identf, self.identf_d)
        cd(self.onesf, self.onesf_d)
        cd(self.identb, self.identb_d)
        cd(self.trib, self.trib_d)
        cd(self.gc, self.gcols)
        cd(self.psc, self.pscale)
        cd(self.kmask, self.kmask_d)
        cd(self.poolfix, self.poolfix_d)
        cd(self.diag8, self.diag8_d)
        cd(self.bfc, self.bfcol)
        cd(self.SELB[0:16, :], self.selb_d)
        tk.dma(tk.sp, self.GFIN[:], self.gfin, writes=[self.bC])
        tk.op(tk.dve, lambda: nc.vector.tensor_scalar(out=self.nbfc, in0=self.bfc, scalar1=-1.0, scalar2=None, op0=ALU.mult),
              reads=[self.bC], writes=[self.bC])
        tk.op(tk.dve, lambda: nc.vector.memset(self.cprev, 0.0), writes=[self.bC])
        tk.op(tk.dve, lambda: nc.vector.memset(self.HALO[:], 0.0), writes=[self.bHALO])
        tk.op(tk.dve, lambda: nc.vector.memset(self.pad0, 0.0), writes=[self.bC])
        tk.op(tk.dve, lambda: nc.vector.memset(self.ONESB[:], 1.0), writes=[self.bC])
        tk.op(tk.dve, lambda: nc.vector.memset(self.CK[:], 0.0), writes=[self.bCK])

    def convert(self, part):
        tk = self.tk

        def conv(dst, src, buf, rows_per):
            n = src.shape[0]
            for r in range(0, n, rows_per):
                tk.dma(tk.pool, dst[r:r + rows_per, :], src[r:r + rows_per, :], writes=[buf], dbuf=buf)
        QC = DFF // 4

        def convq(f, q):
            cs = slice(q * QC, (q + 1) * QC)
            for r in range(0, D, 512):
                tk.dma(tk.pool, self.wg_s[f][r:r + 512, cs], self.w_gate[f][r:r + 512, cs], writes=[self.b_wg[f][q]], dbuf=self.b_wg[f][q])
            for r in range(0, D, 512):
                tk.dma(tk.pool, self.wu_s[f][r:r + 512, cs], self.w_up[f][r:r + 512, cs], writes=[self.b_wu[f][q]], dbuf=self.b_wu[f][q])
            for r in range(q * QC, (q + 1) * QC, 704):
                tk.dma(tk.pool, self.wd_s[f][r:r + 704, :], self.w_down[f][r:r + 704, :], writes=[self.b_wd[f][q]], dbuf=self.b_wd[f][q])
        for f in ([0] if part == 0 else [1]):
            if f == 1:
                conv(self.wp_s, self.w_pool, self.b_wp, 1024)
                conv(self.wo_s, self.w_o, self.b_wo, 512)
            for q in range(4):
                convq(f, q)
            if f == 0:
                conv(self.win_s, self.w_in, self.b_win, 256)

    def load_x(self, slot, nblk=NB):
        tk = self.tk
        for b in range(nblk):
            r0 = slot * T + b * 128
            tk.dma(tk.sp, self.R[:, b * D:(b + 1) * D], self.xs[r0:r0 + 128, :], writes=self.bR[b], dbuf=self.bRld[b])

    def norm_to_xt(self, gidx, nblk=NB, ntok=T, npart=128):
        nc, tk = self.nc, self.tk
        sq = self.TMP[:, 0:1024].bitcast(BF16)
        xnb = self.TMP[:, 1024:2048].bitcast(BF16)
        bsq, bxn = Buf("sq"), Buf("xnb")
        self.tmp_take([bsq, bxn])
        P_ = npart
        sq = sq[0:P_, :]
        xnb = xnb[0:P_, :]
        for b in range(nblk):
            Rb = self.R[0:P_, b * D:(b + 1) * D]
            st = self.ST[0:P_, :]
            tk.op(tk.act, lambda: nc.scalar.activation(out=sq, in_=Rb, func=AF.Square, accum_out=st[:, 3 * b:3 * b + 1]),
                  reads=self.bR[b], writes=[bsq, self.bST])
            tk.op(tk.act, lambda: nc.scalar.activation(out=st[:, 3 * b + 1:3 * b + 2], in_=st[:, 3 * b:3 * b + 1], func=AF.Sqrt,
                                                       scale=1.0 / D, bias=EPS), reads=[self.bST], writes=[self.bST])
            tk.op(tk.dve, lambda: nc.vector.reciprocal(out=st[:, 3 * b + 2:3 * b + 3], in_=st[:, 3 * b + 1:3 * b + 2]),
                  reads=[self.bST], writes=[self.bST])
            tk.op(tk.act, lambda: nc.scalar.activation(out=xnb, in_=Rb, func=AF.Copy, scale=st[:, 3 * b + 2:3 * b + 3]),
                  reads=self.bR[b] + [self.bST], writes=[bxn])
            for half in range(2):
                for j in range(8):
                    kc = half * 8 + j
                    tk.op(tk.pe, lambda: nc.tensor.transpose(out=self.PSB[:, j * 128:j * 128 + P_], in_=xnb[:, kc * 128:(kc + 1) * 128],
                                                             identity=self.identb[0:P_, 0:P_]), reads=[bxn, self.bC], writes=[self.bPS[7]])
                for j in range(8):
                    kc = half * 8 + j
                    tk.op(tk.dve, lambda: nc.vector.tensor_scalar(out=self.XT[:, kc * ntok + b * 128: kc * ntok + b * 128 + P_],
                                                                  in0=self.PSB[:, j * 128:j * 128 + P_],
                                                                  scalar1=self.gc[:, gidx * KC + kc: gidx * KC + kc + 1], scalar2=None,
                                                                  op0=ALU.mult), reads=[self.bPS[7], self.bC], writes=[self.bXT], self_ok=True)

    def ffn_loads(self, f):
        tk = self.tk
        ids = {}
        for grp in range(NFC // 4):
            for j in range(4):
                fc = grp * 4 + j

                def ld(slot, buf, fc=fc):
                    cs = slice(fc * 128, (fc + 1) * 128)
                    tk.dma(tk.sp, slot[:, 0:2048].rearrange("p (k c) -> p k c", c=128),
                           self.wg_s[f][:, cs].rearrange("(k p) c -> p k c", p=128), reads=[self.b_wg[f][fc // 11]], writes=[buf])
                    tk.dma(tk.sp, slot[:, 2048:4096].rearrange("p (k c) -> p k c", c=128),
                           self.wu_s[f][:, cs].rearrange("(k p) c -> p k c", p=128), reads=[self.b_wu[f][fc // 11]], writes=[buf])
                ids[("gu", fc)] = self.add_load(ld)
            if grp >= 1:
                self._ffn_dloads(f, grp - 1, ids)
        self._ffn_dloads(f, NFC // 4 - 1, ids)
        return ids

    def _ffn_dloads(self, f, grp, ids):
        tk = self.tk
        for half in range(2):
            def ld(slot, buf, half=half, grp=grp):
                r0 = (grp * 4 + half * 2) * 128
                tk.dma(tk.sp, slot[:, 0:4096].rearrange("p (j c) -> p j c", c=D),
                       self.wd_s[f][r0:r0 + 256, :].rearrange("(j p) c -> p j c", p=128), reads=[self.b_wd[f][(grp * 4 + half * 2) // 11], self.b_wd[f][(grp * 4 + half * 2 + 1) // 11]], writes=[buf])
            ids[("d", grp, half)] = self.add_load(ld)

    def ffn(self, ids, ncol=T):
        nc, tk = self.nc, self.tk
        sgs = [self.TMP[:, 0:512], self.TMP[:, 512:1024]]
        bsg = [Buf("sg0"), Buf("sg1")]
        self.tmp_take(bsg)
        self.hm_take(self.bHM)
        halves = [(c0, min(512, ncol - c0)) for c0 in range(0, ncol, 512)]
        nblk = (ncol + 127) // 128
        cnt = 0
        dcnt = 0

        def down(grp):
            nonlocal dcnt
            hb = grp % 2
            sa, ba = self.acquire(ids[("d", grp, 0)])
            sb_, bb = self.acquire(ids[("d", grp, 1)])
            for b in range(nblk):
                nt = min(128, ncol - b * 128)
                for slab in range(4):
                    pb = 4 + dcnt % 2
                    dcnt += 1
                    for j in range(4):
                        sl, bsl = (sa, ba) if j < 2 else (sb_, bb)
                        tk.op(tk.pe, lambda: nc.tensor.matmul(self.PS[pb][0:nt, :],
                                                              lhsT=self.HM[:, (hb * 4 + j) * T + b * 128:(hb * 4 + j) * T + b * 128 + nt],
                                                              rhs=sl[:, (j % 2) * D + slab * 512:(j % 2) * D + (slab + 1) * 512],
                                                              start=(j == 0), stop=(j == 3)),
                              reads=[self.bHM[hb], bsl], writes=[self.bPS[pb]], signal=(j == 3))
                    Rs = self.R[0:nt, b * D + slab * 512: b * D + (slab + 1) * 512]
                    tk.op(tk.dve, lambda: nc.vector.scalar_tensor_tensor(out=Rs, in0=self.PS[pb][0:nt, :], scalar=0.5, in1=Rs,
                                                                         op0=ALU.mult, op1=ALU.add),
                          reads=[self.bPS[pb], self.bR[b][slab]], writes=[self.bR[b][slab]])

        for grp in range(NFC // 4):
            hb = grp % 2
            for j in range(4):
                fc = grp * 4 + j
                sl, bsl = self.acquire(ids[("gu", fc)])
                for (c0, n) in halves:
                    pp = cnt % 2
                    cnt += 1
                    pg, pu = 2 * pp, 2 * pp + 1
                    for kc in range(KC):
                        tk.op(tk.pe, lambda: nc.tensor.matmul(self.PS[pg][:, 0:n], lhsT=sl[:, kc * 128:(kc + 1) * 128],
                                                              rhs=self.XT[:, kc * ncol + c0: kc * ncol + c0 + n],
                                                              start=(kc == 0), stop=(kc == KC - 1)),
                              reads=[bsl, self.bXT], writes=[self.bPS[pg]], signal=(kc == KC - 1))
                    for kc in range(KC):
                        tk.op(tk.pe, lambda: nc.tensor.matmul(self.PS[pu][:, 0:n], lhsT=sl[:, 2048 + kc * 128:2048 + (kc + 1) * 128],
                                                              rhs=self.XT[:, kc * ncol + c0: kc * ncol + c0 + n],
                                                              start=(kc == 0), stop=(kc == KC - 1)),
                              reads=[bsl, self.bXT], writes=[self.bPS[pu]], signal=(kc == KC - 1))
                    tk.op(tk.act, lambda: nc.scalar.activation(out=sgs[pp][:, 0:n], in_=self.PS[pg][:, 0:n], func=AF.Silu),
                          reads=[self.bPS[pg]], writes=[bsg[pp]])
                    tk.op(tk.dve, lambda: nc.vector.tensor_tensor(out=self.HM[:, (hb * 4 + j) * T + c0:(hb * 4 + j) * T + c0 + n],
                                                                  in0=sgs[pp][:, 0:n], in1=self.PS[pu][:, 0:n], op=ALU.mult),
                          reads=[bsg[pp], self.bPS[pu]], writes=[self.bHM[hb]], self_ok=True)
            if grp >= 1:
                down(grp - 1)
        down(NFC // 4 - 1)

    def inproj_loads(self, full):
        tk = self.tk
        chunks = [("k", h) for h in range(H)] + [("v", h) for h in range(H)] + [("p", c) for c in range(8)]
        if full:
            chunks += [("q", h) for h in range(H)]
        col0 = {"q": 0, "k": 1024, "v": 2048, "p": 3080}
        ids = {}
        for i in range(0, len(chunks), 2):
            pair = chunks[i:i + 2]

            def ld(slot, buf, pair=pair, first=(i == 0)):
                for j, (kind, idx) in enumerate(pair):
                    c = col0[kind] + idx * 128
                    tk.dma(tk.sp, slot[:, j * 2048:(j + 1) * 2048].rearrange("p (k c) -> p k c", c=128),
                           self.win_s[:, c:c + 128].rearrange("(k p) c -> p k c", p=128), reads=[self.b_win], writes=[buf])
            lid = self.add_load(ld)
            for j, ch in enumerate(pair):
                ids[ch] = (lid, j)
        return ids

    def load_wf(self):
        nc, tk = self.nc, self.tk
        self.WF = nc.alloc_sbuf_tensor("WF", [128, KC * 8], BF16)
        wf32 = nc.alloc_sbuf_tensor("WF32", [128, KC * 8], F32)
        b32 = Buf("wf32")
        tk.dma(tk.sp, wf32[:], self.wf_d, writes=[b32])
        tk.op(tk.dve, lambda: nc.vector.tensor_copy(out=self.WF[:], in_=wf32[:]), reads=[b32], writes=[self.bC])

    def inproj(self, slot, ids, full, last_full):
        nc, tk = self.nc, self.tk
        hm = self.HM
        kts = [hm[:, 0:1024], hm[:, 1024:2048]]
        kf = hm[:, 2048:4096].bitcast(F32)
        outk = [hm[:, 4096:6144].bitcast(F32), hm[:, 6144:8192].bitcast(F32)]
        bkts = [Buf("kts0"), Buf("kts1")]
        bkf = Buf("kf")
        boutk = [Buf("outk0"), Buf("outk1")]
        self.hm_take(bkts + [bkf] + boutk)
        if not getattr(self, "wf_loaded", False):
            self.load_wf()
            self.wf_loaded = True
        tmp = self.TMP
        kfs = [kf, tmp[:, 0:1024]]
        bkfs = [bkf, Buf("kf2")]
        self.tmp_take([bkfs[1]])
        cnt = 0

        def proj(w_ap, bw, c0, n, pbank, m=128, wcols=128):
            for kc in range(KC):
                tk.op(tk.pe, lambda: nc.tensor.matmul(self.PS[pbank][0:m, 0:n], lhsT=w_ap[:, kc * wcols: kc * wcols + m],
                                                      rhs=self.XT[:, kc * T + c0: kc * T + c0 + n],
                                                      start=(kc == 0), stop=(kc == KC - 1)),
                      reads=[bw, self.bXT], writes=[self.bPS[pbank]], signal=(kc == KC - 1))

        items = [("k", h) for h in range(H)] + [("v", h) for h in range(H)]

        def stage_a(i):
            nonlocal cnt
            kind, h = items[i]
            lid, j = ids[(kind, h)]
            sl, bsl = self.acquire(lid)
            w_ap = sl[:, j * 2048:(j + 1) * 2048]
            kb_ = i % 2
            for half in range(2):
                pb = cnt % 2
                cnt += 1
                proj(w_ap, bsl, half * 512, 512, pb)
                tk.op(tk.dve, lambda: nc.vector.tensor_copy(out=kfs[kb_][:, half * 512:(half + 1) * 512], in_=self.PS[pb][:, :]),
                      reads=[self.bPS[pb]], writes=[bkfs[kb_]], self_ok=True)
                if kind == "k":
                    tk.op(tk.act, lambda: nc.scalar.activation(out=kts[kb_][:, half * 512:(half + 1) * 512],
                                                               in_=kfs[kb_][:, half * 512:(half + 1) * 512], func=AF.Copy),
                          reads=[bkfs[kb_]], writes=[bkts[kb_]], self_ok=True)
            if kind == "k":
                r0 = (slot * H + h) * 128
                tk.dma(tk.pool, self.kt_hist[r0:r0 + 128, :], kts[kb_], reads=[bkts[kb_]], writes=[self.b_kth[slot][h]], dbuf=bkts[kb_])

        def stage_b(i):
            kind, h = items[i]
            kb_ = i % 2
            for q4 in range(2):
                pt = 2 + q4
                for jj in range(4):
                    b = q4 * 4 + jj
                    tk.op(tk.pe, lambda: nc.tensor.transpose(out=self.PS[pt][:, jj * 128:(jj + 1) * 128], in_=kfs[kb_][:, b * 128:(b + 1) * 128],
                                                             identity=self.identf), reads=[bkfs[kb_], self.bC], writes=[self.bPS[pt]])
                tk.op(tk.act, lambda: nc.scalar.activation(out=outk[kb_][:, q4 * 512:(q4 + 1) * 512], in_=self.PS[pt][:, :], func=AF.Copy),
                      reads=[self.bPS[pt]], writes=[boutk[kb_]], self_ok=True)
                if kind == "v":
                    tk.op(tk.pool, lambda: nc.gpsimd.tensor_copy(out=kts[kb_][:, q4 * 512:(q4 + 1) * 512], in_=outk[kb_][:, q4 * 512:(q4 + 1) * 512]),
                          reads=[boutk[kb_]], writes=[bkts[kb_]])
            dst = self.k_out if kind == "k" else self.v_out
            tk.dma(tk.pool, dst.rearrange("(s b p) c -> s p b c", p=128, b=NB)[slot, :, :, h * 128:(h + 1) * 128],
                   outk[kb_].rearrange("p (b c) -> p b c", c=128), reads=[boutk[kb_]], writes=[self.b_out], dbuf=boutk[kb_])
            if kind == "v":
                r0 = (slot * H + h) * 128
                tk.dma(tk.pool, self.v_hist[r0:r0 + 128, :], kts[kb_], reads=[bkts[kb_]], writes=[self.b_vh[slot][h]], dbuf=bkts[kb_])

        for i in range(len(items)):
            stage_a(i)
            if i >= 1:
                stage_b(i - 1)
        stage_b(len(items) - 1)

        lf = tmp[0:8, 0:1024]
        ct = tmp[0:8, 1024:2048]
        lfo = tmp[:, 2048:2112]
        cto = tmp[:, 2112:2176]
        rhs8 = tmp[0:8, 2176:2240]
        blf, bct, blfo, bcto, brhs8 = Buf("lf"), Buf("ct"), Buf("lfo"), Buf("cto"), Buf("rhs8")
        bones = Buf("ones8")
        self.tmp_take([blf, bct, blfo, bcto, brhs8, bones])
        self.bones = bones
        import os
        ksub = int(os.environ.get("KSUB", "99"))
        if ksub == 1:
            return
        for half in range(2):
            pb = cnt % 2
            cnt += 1
            proj(self.WF[:], self.bC, half * 512, 512, pb, m=8, wcols=8)
            hs = slice(half * 512, (half + 1) * 512)
            tk.op(tk.act, lambda: nc.scalar.activation(out=lf[:, hs], in_=self.PS[pb][0:8, :], func=AF.Exp, scale=-1.0, bias=self.nbfc),
                  reads=[self.bPS[pb], self.bC], writes=[blf])
            tk.op(tk.act, lambda: nc.scalar.activation(out=lf[:, hs], in_=lf[:, hs], func=AF.Ln, scale=1.0, bias=1.0),
                  reads=[blf], writes=[blf])
            tk.op(tk.dve, lambda: nc.vector.tensor_scalar(out=lf[:, hs], in0=lf[:, hs], scalar1=-1.0, scalar2=None, op0=ALU.mult),
                  reads=[blf], writes=[blf])
        if ksub == 2:
            return
        self.cumsum_logf(slot, lf, ct, lfo, cto, rhs8, blf, bct, blfo, bcto, brhs8, T, want_hs=full)
        if ksub == 3:
            return
        if full:
            self.pool_chunks(slot, ids, last_full)
        else:
            for c in range(8):
                lid, j = ids[("p", c)]
                sl, bsl = self.acquire(lid)
                pb = cnt % 2
                cnt += 1
                proj(sl[:, j * 2048:(j + 1) * 2048], bsl, T - 128, 128, pb)
                tk.op(tk.act, lambda: nc.scalar.activation(out=self.HALO[:, c * 16 + 1:c * 16 + 16], in_=self.PS[pb][:, 113:128], func=AF.Copy),
                      reads=[self.bPS[pb]], writes=[self.bHALO])
        if full:
            self.qt_take(self.bQT)
            for h in range(H):
                lid, j = ids[("q", h)]
                sl, bsl = self.acquire(lid)
                for half in range(2):
                    pb = cnt % 2
                    cnt += 1
                    proj(sl[:, j * 2048:(j + 1) * 2048], bsl, half * 512, 512, pb)
                    tk.op(tk.act, lambda: nc.scalar.activation(out=self.QT[:, h * T + half * 512: h * T + (half + 1) * 512], in_=self.PS[pb][:, :],
                                                               func=AF.Copy), reads=[self.bPS[pb]], writes=[self.bQT[h]])

    def cumsum_logf(self, slot, lf, ct, lfo, cto, rhs8, blf, bct, blfo, bcto, brhs8, ntok, want_hs=False):
        nc, tk = self.nc, self.tk
        nblk = ntok // 128
        ones = self.TMP[0:8, 2240:2240 + 512]
        bones = self.bones
        tk.op(tk.dve, lambda: nc.vector.memset(ones, 1.0), writes=[bones])
        for c0 in range(0, ntok, 512):
            n = min(512, ntok - c0)
            tk.op(tk.dve, lambda: nc.vector.tensor_tensor_scan(out=ct[:, c0:c0 + n], data0=ones[:, 0:n], data1=lf[:, c0:c0 + n],
                                                               initial=self.cprev, op0=ALU.mult, op1=ALU.add),
                  reads=[blf, bones, self.bC], writes=[bct])
            tk.op(tk.dve, lambda: nc.vector.tensor_copy(out=self.cprev, in_=ct[:, c0 + n - 1:c0 + n]), reads=[bct], writes=[self.bC])
        for b in range(nblk):
            tk.op(tk.pe, lambda: nc.tensor.transpose(out=self.PS[2][:, b * 8:(b + 1) * 8], in_=lf[:, b * 128:(b + 1) * 128],
                                                     identity=self.identf[0:8, 0:8]), reads=[blf, self.bC], writes=[self.bPS[2]])
            tk.op(tk.pe, lambda: nc.tensor.transpose(out=self.PS[3][:, b * 8:(b + 1) * 8], in_=ct[:, b * 128:(b + 1) * 128],
                                                     identity=self.identf[0:8, 0:8]), reads=[bct, self.bC], writes=[self.bPS[3]])
        tk.op(tk.act, lambda: nc.scalar.activation(out=lfo[:, 0:nblk * 8], in_=self.PS[2][:, 0:nblk * 8], func=AF.Copy),
              reads=[self.bPS[2]], writes=[blfo])
        if want_hs:
            h1t = ones.bitcast(BF16)
            tk.op(tk.dve, lambda: nc.vector.tensor_scalar(out=lf, in0=ct, scalar1=1.0 / SCALE, scalar2=None, op0=ALU.mult),
                  reads=[bct], writes=[blf])
            tk.op(tk.dve, lambda: nc.vector.tensor_copy(out=self.HS[0:8, :], in_=lf), reads=[blf], writes=[self.bHS])
            tk.op(tk.dve, lambda: nc.vector.tensor_tensor(out=lf, in0=lf, in1=self.HS[0:8, :], op=ALU.subtract), reads=[blf, self.bHS], writes=[blf])
            tk.op(tk.dve, lambda: nc.vector.tensor_copy(out=h1t, in_=lf), reads=[blf], writes=[bones])
            tk.dma(tk.sp, self.HS[8:16, :], h1t, reads=[bones], writes=[self.bHS], dbuf=bones)
        if slot is not None:
            tk.dma(tk.pool, self.lf_out[slot * 128:(slot + 1) * 128, :], lfo, reads=[blfo], writes=[self.b_out], dbuf=blfo)
            tk.op(tk.dve, lambda: nc.vector.tensor_copy(
                out=self.CK[:].rearrange("p (h k) -> p k h", k=64)[:, slot * 8:(slot + 1) * 8, :],
                in_=self.PS[3][:, 0:64].rearrange("p (b h) -> p b h", h=8)), reads=[self.bPS[3]], writes=[self.bCK])
            for h in range(H):
                tk.op(tk.dve, lambda: nc.vector.tensor_tensor(out=rhs8[:, h * 8:(h + 1) * 8], in0=self.diag8[:, h * 8:(h + 1) * 8],
                                                              in1=ct[:, 127:ntok:128], op=ALU.mult), reads=[bct, self.bC], writes=[brhs8])
            tk.op(tk.pe, lambda: nc.tensor.matmul(self.PS[2][:, 64:128], lhsT=self.onesf[0:8, :], rhs=rhs8, start=True, stop=True),
                  reads=[brhs8, self.bC], writes=[self.bPS[2]])
            tk.op(tk.dve, lambda: nc.vector.tensor_copy(out=self.CR[:], in_=self.PS[2][:, 64:128]), reads=[self.bPS[2]], writes=[self.bCR])

    def pool_chunks(self, slot, ids, last_full):
        nc, tk = self.nc, self.tk
        qt = self.QT
        W = 16 + 512
        p0 = qt[:, 0:2 * W].bitcast(F32)
        tb = qt[:, 2 * W:4 * W].bitcast(F32)
        tc_ = qt[:, 4 * W:6 * W].bitcast(F32)
        bp0, btb, btc = Buf("p0"), Buf("tb"), Buf("tc")
        self.qt_take([bp0, btb, btc])
        db = self.HM
        bdb = [Buf("db")]
        first = True
        cnt = 0
        for c in range(8):
            g = c // 2
            w = 2 << g
            lid, j = ids[("p", c)]
            sl, bsl = self.acquire(lid)
            for half in range(2):
                pb = cnt % 2
                cnt += 1
                for kc in range(KC):
                    tk.op(tk.pe, lambda: nc.tensor.matmul(self.PS[pb][:, :], lhsT=sl[:, j * 2048 + kc * 128: j * 2048 + (kc + 1) * 128],
                                                          rhs=self.XT[:, kc * T + half * 512: kc * T + (half + 1) * 512],
                                                          start=(kc == 0), stop=(kc == KC - 1)),
                          reads=[bsl, self.bXT], writes=[self.bPS[pb]], signal=(kc == KC - 1))
                if half == 0:
                    tk.op(tk.dve, lambda: nc.vector.tensor_copy(out=p0[:, 0:16], in_=self.HALO[:, c * 16:(c + 1) * 16]),
                          reads=[self.bHALO], writes=[bp0])
                else:
                    tk.op(tk.dve, lambda: nc.vector.tensor_copy(out=p0[:, 0:16], in_=p0[:, 512:528]), reads=[bp0], writes=[bp0])
                tk.op(tk.act, lambda: nc.scalar.activation(out=p0[:, 16:W], in_=self.PS[pb][:, :], func=AF.Copy),
                      reads=[self.bPS[pb]], writes=[bp0])
                src, bsrc = p0, bp0
                step = 1
                k = 0
                while step < w:
                    dst, bdst = (tb, btb) if k % 2 == 0 else (tc_, btc)
                    lo = 2 * step
                    tk.op(tk.dve, lambda: nc.vector.tensor_tensor(out=dst[:, lo:W], in0=src[:, lo:W], in1=src[:, lo - step:W - step], op=ALU.add),
                          reads=[bsrc], writes=[bdst])
                    src, bsrc = dst, bdst
                    step *= 2
                    k += 1
                if slot == 1 and half == 0:
                    tk.op(tk.dve, lambda: nc.vector.tensor_tensor(out=src[:, 16:32], in0=src[:, 16:32], in1=self.poolfix[:, g * 16:(g + 1) * 16],
                                                                  op=ALU.mult), reads=[bsrc, self.bC], writes=[bsrc])
                if first:
                    self.hm_take(bdb)
                    first = False
                tk.op(tk.dve, lambda: nc.vector.scalar_tensor_tensor(out=db[:, c * T + half * 512: c * T + (half + 1) * 512], in0=src[:, 16:W],
                                                                     scalar=1.0 / w, in1=p0[:, 16:W], op0=ALU.mult, op1=ALU.subtract),
                      reads=[bsrc, bp0], writes=bdb)
                if last_full and half == 1:
                    tk.op(tk.pe, lambda: nc.tensor.transpose(out=self.PS[2][0:15, c * 128:(c + 1) * 128] if c < 4 else self.PS[3][0:15, (c - 4) * 128:(c - 3) * 128],
                                                             in_=p0[:, W - 15:W], identity=self.identf),
                          reads=[bp0, self.bC], writes=[self.bPS[2] if c < 4 else self.bPS[3]])
        if last_full:
            pso = self.TMP[0:15, 2048:3072]
            bpso = Buf("pso")
            self.tmp_take([bpso])
            tk.op(tk.act, lambda: nc.scalar.activation(out=pso[:, 0:512], in_=self.PS[2][0:15, :], func=AF.Copy), reads=[self.bPS[2]], writes=[bpso])
            tk.op(tk.act, lambda: nc.scalar.activation(out=pso[:, 512:1024], in_=self.PS[3][0:15, :], func=AF.Copy), reads=[self.bPS[3]], writes=[bpso])
            tk.dma(tk.pool, self.pool_out, pso, reads=[bpso], writes=[self.b_out], dbuf=bpso)
        self.bDB = bdb[0]

    def attn_loads(self, s):
        tk = self.tk
        ids = {}
        for h in range(H):
            for ks in range(s + 1):
                def ld(slot, buf, h=h, ks=ks):
                    r0 = (ks * H + h) * 128
                    tk.dma(tk.sp, slot[:, 0:1024], self.kt_hist[r0:r0 + 128, :], reads=[self.b_kth[ks][h]], writes=[buf])
                    tk.dma(tk.sp, slot[:, 1024:2048], self.v_hist[r0:r0 + 128, :], reads=[self.b_vh[ks][h]], writes=[buf])
                ids[(h, ks)] = self.add_load(ld)
        return ids

    def attention(self, s, ids):
        nc, tk = self.nc, self.tk
        rl = self.TMP[:, 0:512]
        brl = Buf("rl")
        self.tmp_take([brl])
        iters = []
        for h in range(H):
            for ks in range(s + 1):
                for kb in range(NB):
                    for half in range(2):
                        qb0 = half * 4
                        if ks == s:
                            if kb > qb0 + 3:
                                continue
                            first_q = max(kb, qb0)
                        else:
                            first_q = qb0
                        last = (ks == s) and (kb == min(NB - 1, qb0 + 3))
                        iters.append(dict(h=h, ks=ks, kb=kb, half=half, qb0=qb0, first_q=first_q, col0=(first_q - qb0) * 128, last=last,
                                          headend=(ks == s and kb == NB - 1 and half == 1)))
        state = {"sl": {}, "head": None}
        started = {}

        def get_slot(h, ks):
            key = (h, ks)
            if key not in state["sl"]:
                state["sl"] = {key: self.acquire(ids[key])}
            return state["sl"][key]

        def emit_scores(i, it):
            h, ks, kb, half, col0 = it["h"], it["ks"], it["kb"], it["half"], it["col0"]
            sl, bsl = get_slot(h, ks)
            it["sl"], it["bsl"] = sl, bsl
            bk = h % 2
            if state["head"] != h:
                tk.op(tk.dve, lambda: nc.vector.tensor_tensor(out=self.BTK[:, bk * 64:(bk + 1) * 64], in0=self.kmask[:, 0:64],
                                                              in1=self.CK[:, h * 64:(h + 1) * 64], op=ALU.subtract),
                      reads=[self.bCK, self.bC], writes=[self.bBTK[bk]], self_ok=True)
                state["head"] = h
            ps = i % 2
            tk.op(tk.pe, lambda: nc.tensor.matmul(self.PS[ps][:, col0:512], lhsT=sl[:, kb * 128:(kb + 1) * 128],
                                                  rhs=self.QT[:, h * T + half * 512 + col0: h * T + (half + 1) * 512],
                                                  start=True, stop=False), reads=[bsl, self.bQT[h]], writes=[self.bPS[ps]], signal=False)
            tk.op(tk.pe, lambda: nc.tensor.matmul(self.PS[ps][:, col0:512], lhsT=self.SELB[0:16, h * 128:(h + 1) * 128],
                                                  rhs=self.HS[0:16, half * 512 + col0:(half + 1) * 512],
                                                  start=False, stop=True), reads=[self.bC, self.bHS], writes=[self.bPS[ps]], signal=True)

        def emit_rest(i, it):
            h, ks, kb, half, col0, qb0, first_q = it["h"], it["ks"], it["kb"], it["half"], it["col0"], it["qb0"], it["first_q"]
            sl, bsl = it["sl"], it["bsl"]
            bk = h % 2
            gkb = ks * NB + kb
            ps = i % 2
            pt = i % 3
            ptile = self.PT[:, pt * 512:(pt + 1) * 512]
            tk.op(tk.act, lambda: nc.scalar.activation(out=ptile[:, col0:512], in_=self.PS[ps][:, col0:512], func=AF.Exp, scale=SCALE,
                                                       bias=self.BTK[:, bk * 64 + gkb:bk * 64 + gkb + 1]),
                  reads=[self.bPS[ps], self.bBTK[bk]], writes=[self.bPT[pt]], self_ok=True)
            if ks == s and first_q == kb:
                tk.op(tk.pool, lambda: nc.gpsimd.tensor_tensor(out=ptile[:, col0:col0 + 128], in0=ptile[:, col0:col0 + 128],
                                                               in1=self.trib, op=ALU.mult), reads=[self.bPT[pt], self.bC], writes=[self.bPT[pt]])
            st_ = not started.get((h, half), False)
            tk.op(tk.pe, lambda: nc.tensor.matmul(self.PS[2 + half][:, col0:512], lhsT=self.ONESB[:], rhs=ptile[:, col0:512],
                                                  start=st_, stop=it["last"]),
                  reads=[self.bC, self.bPT[pt]], writes=[self.bPS[2 + half]], signal=False)
            tk.op(tk.pe, lambda: nc.tensor.matmul(self.PS[4 + half][:, col0:512], lhsT=sl[:, 1024 + kb * 128:1024 + (kb + 1) * 128],
                                                  rhs=ptile[:, col0:512], start=st_, stop=it["last"]),
                  reads=[bsl, self.bPT[pt]], writes=[self.bPS[4 + half]], signal=True)
            started[(h, half)] = True
            if it["headend"]:
                for hf in range(2):
                    tk.op(tk.dve, lambda: nc.vector.reciprocal(out=rl, in_=self.PS[2 + hf][:, :]), reads=[self.bPS[2 + hf]], writes=[brl])
                    tk.op(tk.dve, lambda: nc.vector.tensor_tensor(out=self.XT[:, h * T + hf * 512: h * T + (hf + 1) * 512],
                                                                  in0=self.PS[4 + hf][:, :], in1=rl, op=ALU.mult),
                          reads=[self.bPS[4 + hf], brl], writes=[self.bXT])

        emit_scores(0, iters[0])
        for i, it in enumerate(iters):
            if i + 1 < len(iters):
                emit_scores(i + 1, iters[i + 1])
            emit_rest(i, it)

    def mix_loads(self):
        tk = self.tk
        ids = {}

        def ldp(slot, buf):
            tk.dma(tk.sp, slot[:, 0:2048].rearrange("p (a c) -> p a c", c=256),
                   self.wp_s.rearrange("(a p) c -> p a c", p=128), reads=[self.b_wp], writes=[buf])
        ids["wp"] = self.add_load(ldp)
        for slab in range(8):
            def ld(slot, buf, slab=slab):
                tk.dma(tk.sp, slot[:, 0:4096].rearrange("p (k c) -> p k c", c=256),
                       self.wo_s[:, slab * 256:(slab + 1) * 256].rearrange("(k p) c -> p k c", p=128), reads=[self.b_wo], writes=[buf])
            ids[("wo", slab)] = self.add_load(ld)
        return ids

    def mix(self, ids, ncol=T, db=None):
        nc, tk = self.nc, self.tk
        if db is None:
            db = self.HM[:]
        sl, bsl = self.acquire(ids["wp"])
        cnt = 0
        halves = [(c0, min(512, ncol - c0)) for c0 in range(0, ncol, 512)]
        nblk = (ncol + 127) // 128
        for g in range(4):
            for oc in range(2):
                for (c0, n) in halves:
                    pb = cnt % 2
                    cnt += 1
                    for k2 in range(2):
                        a = g * 2 + k2
                        tk.op(tk.pe, lambda: nc.tensor.matmul(self.PS[pb][:, 0:n], lhsT=sl[:, a * 256 + oc * 128: a * 256 + (oc + 1) * 128],
                                                              rhs=db[:, (g * 2 + k2) * ncol + c0:(g * 2 + k2) * ncol + c0 + n],
                                                              start=(k2 == 0), stop=(k2 == 1)),
                              reads=[bsl, self.bDB], writes=[self.bPS[pb]], signal=(k2 == 1))
                    ch = 8 + g * 2 + oc
                    tk.op(tk.act, lambda: nc.scalar.activation(out=self.XT[:, ch * ncol + c0: ch * ncol + c0 + n], in_=self.PS[pb][:, 0:n],
                                                               func=AF.Copy, scale=self.psc[:, g * 2 + oc: g * 2 + oc + 1]),
                          reads=[self.bPS[pb], self.bC], writes=[self.bXT])
        for slab in range(8):
            sl, bsl = self.acquire(ids[("wo", slab)])
            for b in range(nblk):
                nt = min(128, ncol - b * 128)
                pb = 2 + cnt % 2
                cnt += 1
                for kc in range(KC):
                    tk.op(tk.pe, lambda: nc.tensor.matmul(self.PS[pb][0:nt, 0:256], lhsT=self.XT[:, kc * ncol + b * 128: kc * ncol + b * 128 + nt],
                                                          rhs=sl[:, kc * 256:(kc + 1) * 256], start=(kc == 0), stop=(kc == KC - 1)),
                          reads=[bsl, self.bXT], writes=[self.bPS[pb]], signal=(kc == KC - 1))
                Rs = self.R[0:nt, b * D + slab * 256: b * D + (slab + 1) * 256]
                tk.op(tk.dve, lambda: nc.vector.tensor_tensor(out=Rs, in0=Rs, in1=self.PS[pb][0:nt, 0:256], op=ALU.add),
                      reads=[self.bPS[pb], self.bR[b][slab // 2]], writes=[self.bR[b][slab // 2]])

    def final_out(self, dst_rows, nblk=NB, npart=128):
        nc, tk = self.nc, self.tk
        sq = self.TMP[:, 0:1024].bitcast(BF16)
        bsq = Buf("sqf")
        self.tmp_take([bsq])
        st = self.ST[0:npart, :]
        sq = sq[0:npart, :]
        for b in range(nblk):
            Rb = self.R[0:npart, b * D:(b + 1) * D]
            tk.op(tk.act, lambda: nc.scalar.activation(out=sq, in_=Rb, func=AF.Square, accum_out=st[:, 3 * b:3 * b + 1]),
                  reads=self.bR[b], writes=[bsq, self.bST])
            tk.op(tk.act, lambda: nc.scalar.activation(out=st[:, 3 * b + 1:3 * b + 2], in_=st[:, 3 * b:3 * b + 1], func=AF.Sqrt,
                                                       scale=1.0 / D, bias=EPS), reads=[self.bST], writes=[self.bST])
            tk.op(tk.dve, lambda: nc.vector.reciprocal(out=st[:, 3 * b + 2:3 * b + 3], in_=st[:, 3 * b + 1:3 * b + 2]),
                  reads=[self.bST], writes=[self.bST])
            tk.op(tk.dve, lambda: nc.vector.scalar_tensor_tensor(out=Rb, in0=Rb, scalar=st[:, 3 * b + 2:3 * b + 3], in1=self.GFIN[0:npart, :],
                                                                 op0=ALU.mult, op1=ALU.mult), reads=self.bR[b] + [self.bST, self.bC], writes=self.bR[b])
            tk.dma(tk.pool, dst_rows(b), Rb, reads=self.bR[b], writes=[self.b_out], dbuf=self.bRst[b])

    def sample_attn_loads(self):
        tk = self.tk
        ids = {}
        for h in range(H):
            for sq_ in range(2):
                def ld(slot, buf, h=h, sq_=sq_):
                    r0 = (sq_ * H + h) * 128
                    tk.dma(tk.sp, slot[:, 0:2048].bitcast(F32), self.ckT[r0:r0 + 128, :], writes=[buf])
                    tk.dma(tk.sp, slot[:, 2048:4096].bitcast(F32), self.cv[r0:r0 + 128, :], writes=[buf])
                ids[(h, sq_)] = self.add_load(ld)
        return ids

    def sample_slot(self, plan):
        nc, tk = self.nc, self.tk
        NS = 64
        tk.dma(tk.sp, self.R[0:64, 0:D], self.xsmp, writes=self.bR[0], dbuf=self.bRld[0])
        self.norm_to_xt(0, nblk=1, ntok=NS, npart=NS)
        self.ffn(plan["ffn1"], ncol=NS)
        self.norm_to_xt(1, nblk=1, ntok=NS, npart=NS)
        ids = plan["inproj"]
        hm = self.HM
        kf = hm[:, 0:128].bitcast(F32)
        kn = hm[:, 128:640]
        vn = hm[:, 640:1664]
        kout = hm[:, 1664:3712].bitcast(F32)
        vout = hm[:, 3712:5760].bitcast(F32)
        dbs = hm[:, 5760:6272]
        bkf, bkn, bvn, bkout, bvout, bdbs = Buf("skf"), Buf("skn"), Buf("svn"), Buf("skout"), Buf("svout"), Buf("sdbs")
        self.hm_take([bkf, bkn, bvn, bkout, bvout, bdbs])
        if not getattr(self, "wf_loaded", False):
            self.load_wf()
            self.wf_loaded = True
        tmp = self.TMP
        cc = [tmp[0:8, 0:1024], tmp[0:8, 1024:2048]]
        ones = tmp[0:8, 2048:3072]
        bcc, bones = Buf("scc"), Buf("sones")
        self.tmp_take([bcc, bones])
        smp = self.SMP
        lfn = smp[0:8, 0:64]
        cn = smp[0:8, 64:128]
        rhs8 = smp[0:8, 128:144]
        cks = smp[:, 144:272]
        crs = smp[:, 272:288]
        cnT = smp[0:64, 288:296]
        crn = smp[0:64, 296:304]
        btc = smp[:, 304:432]
        btn = smp[0:64, 432:440]
        lfs = smp[0:64, 440:448]
        bsm = Buf("smp")
        bspt, bm64 = Buf("spt"), Buf("m64")
        tk.dma(tk.sp, self.SPT[:], self.spT, writes=[bspt])
        tk.dma(tk.sp, self.M64[0:64, :], self.mask64_d, writes=[bm64])
        tk.dma(tk.sp, cc[0], self.clfT[0:8, :], writes=[bcc])
        tk.dma(tk.sp, cc[1], self.clfT[8:16, :], writes=[bcc])
        tk.op(tk.dve, lambda: nc.vector.memset(ones, 1.0), writes=[bones])
        cnt = 0

        def proj(w_ap, bw, pbank, m=128, wcols=128):
            for kc in range(KC):
                tk.op(tk.pe, lambda: nc.tensor.matmul(self.PS[pbank][0:m, 0:NS], lhsT=w_ap[:, kc * wcols: kc * wcols + m],
                                                      rhs=self.XT[:, kc * NS:(kc + 1) * NS], start=(kc == 0), stop=(kc == KC - 1)),
                      reads=[bw, self.bXT], writes=[self.bPS[pbank]], signal=(kc == KC - 1))
        for kind in ("k", "v"):
            for h in range(H):
                lid, j = ids[(kind, h)]
                sl, bsl = self.acquire(lid)
                pb = cnt % 2
                cnt += 1
                proj(sl[:, j * 2048:(j + 1) * 2048], bsl, pb)
                tk.op(tk.dve, lambda: nc.vector.tensor_copy(out=kf, in_=self.PS[pb][:, 0:NS]), reads=[self.bPS[pb]], writes=[bkf])
                if kind == "k":
                    tk.op(tk.act, lambda: nc.scalar.activation(out=kn[:, h * NS:(h + 1) * NS], in_=kf, func=AF.Copy), reads=[bkf], writes=[bkn])
                pt = 2 + h // 4
                tk.op(tk.pe, lambda: nc.tensor.transpose(out=self.PS[pt][0:NS, (h % 4) * 128:(h % 4 + 1) * 128], in_=kf, identity=self.identf),
                      reads=[bkf, self.bC], writes=[self.bPS[pt]])
            o, bo = (kout, bkout) if kind == "k" else (vout, bvout)
            for q4 in range(2):
                tk.op(tk.act, lambda: nc.scalar.activation(out=o[0:NS, q4 * 512:(q4 + 1) * 512], in_=self.PS[2 + q4][0:NS, :], func=AF.Copy),
                      reads=[self.bPS[2 + q4]], writes=[bo])
            tk.dma(tk.pool, self.ks_out if kind == "k" else self.vs_out, o[0:NS, :], reads=[bo], writes=[self.b_out], dbuf=bo)
            if kind == "v":
                tk.op(tk.pool, lambda: nc.gpsimd.tensor_copy(out=vn[0:NS, :], in_=vout[0:NS, :]), reads=[bvout], writes=[bvn])
        pb = cnt % 2
        cnt += 1
        proj(self.WF[:], self.bC, pb, m=8, wcols=8)
        tk.op(tk.act, lambda: nc.scalar.activation(out=lfn, in_=self.PS[pb][0:8, 0:NS], func=AF.Exp, scale=-1.0, bias=self.nbfc),
              reads=[self.bPS[pb], self.bC], writes=[bsm])
        tk.op(tk.act, lambda: nc.scalar.activation(out=lfn, in_=lfn, func=AF.Ln, scale=1.0, bias=1.0), reads=[bsm], writes=[bsm])
        tk.op(tk.dve, lambda: nc.vector.tensor_scalar(out=lfn, in0=lfn, scalar1=-1.0, scalar2=None, op0=ALU.mult), reads=[bsm], writes=[bsm])
        for sq_ in range(2):
            for c0 in range(0, 1024, 512):
                init = 0.0 if c0 == 0 else cc[sq_][:, c0 - 1:c0]
                tk.op(tk.dve, lambda: nc.vector.tensor_tensor_scan(out=cc[sq_][:, c0:c0 + 512], data0=ones[:, 0:512], data1=cc[sq_][:, c0:c0 + 512],
                                                                   initial=init, op0=ALU.mult, op1=ALU.add), reads=[bcc, bones], writes=[bcc])
            tk.op(tk.dve, lambda: nc.vector.tensor_tensor_scan(out=cn[:, sq_ * 32:(sq_ + 1) * 32], data0=ones[:, 0:32], data1=lfn[:, sq_ * 32:(sq_ + 1) * 32],
                                                               initial=cc[sq_][:, 1023:1024], op0=ALU.mult, op1=ALU.add),
                  reads=[bcc, bones, bsm], writes=[bsm])
            for kb in range(NB):
                tk.op(tk.pe, lambda: nc.tensor.transpose(out=self.PS[3][:, (sq_ * 8 + kb) * 8:(sq_ * 8 + kb + 1) * 8], in_=cc[sq_][:, kb * 128:(kb + 1) * 128],
                                                         identity=self.identf[0:8, 0:8]), reads=[bcc, self.bC], writes=[self.bPS[3]])
            tk.op(tk.dve, lambda: nc.vector.tensor_scalar(out=rhs8[:, sq_ * 8:(sq_ + 1) * 8], in0=self.identf[0:8, 0:8],
                                                          scalar1=cn[:, sq_ * 32 + 31:sq_ * 32 + 32], scalar2=None, op0=ALU.mult),
                  reads=[bsm, self.bC], writes=[bsm])
        tk.op(tk.dve, lambda: nc.vector.tensor_copy(out=cks, in_=self.PS[3][:, 0:128]), reads=[self.bPS[3]], writes=[bsm])
        tk.op(tk.pe, lambda: nc.tensor.matmul(self.PS[2][:, 0:16], lhsT=self.onesf[0:8, :], rhs=rhs8, start=True, stop=True),
              reads=[bsm, self.bC], writes=[self.bPS[2]])
        tk.op(tk.dve, lambda: nc.vector.tensor_copy(out=crs, in_=self.PS[2][:, 0:16]), reads=[self.bPS[2]], writes=[bsm])
        tk.op(tk.pe, lambda: nc.tensor.transpose(out=self.PS[3][0:NS, 128:136], in_=cn, identity=self.identf[0:8, 0:8]),
              reads=[bsm, self.bC], writes=[self.bPS[3]])
        tk.op(tk.pe, lambda: nc.tensor.transpose(out=self.PS[3][0:NS, 136:144], in_=lfn, identity=self.identf[0:8, 0:8]),
              reads=[bsm, self.bC], writes=[self.bPS[3]])
        tk.op(tk.dve, lambda: nc.vector.tensor_copy(out=cnT, in_=self.PS[3][0:NS, 128:136]), reads=[self.bPS[3]], writes=[bsm])
        tk.op(tk.dve, lambda: nc.vector.tensor_copy(out=lfs, in_=self.PS[3][0:NS, 136:144]), reads=[self.bPS[3]], writes=[bsm])
        tk.dma(tk.pool, self.lfs_out, lfs, reads=[bsm], writes=[self.b_out], dbuf=bsm)
        tk.op(tk.dve, lambda: nc.vector.tensor_copy(out=crn[0:32, :], in_=crs[0:32, 0:8]), reads=[bsm], writes=[bsm])
        tk.op(tk.dve, lambda: nc.vector.tensor_copy(out=crn[32:64, :], in_=crs[32:64, 8:16]), reads=[bsm], writes=[bsm])
        tk.op(tk.dve, lambda: nc.vector.tensor_tensor(out=btn, in0=crn, in1=cnT, op=ALU.subtract), reads=[bsm], writes=[bsm])
        for sq_ in range(2):
            for kb in range(NB):
                o_ = (sq_ * 8 + kb) * 8
                tk.op(tk.dve, lambda: nc.vector.tensor_tensor(out=btc[:, o_:o_ + 8], in0=crs[:, sq_ * 8:(sq_ + 1) * 8], in1=cks[:, o_:o_ + 8],
                                                              op=ALU.subtract), reads=[bsm], writes=[bsm])
        qt = self.QT
        W = 96
        p0 = qt[:, 0:2 * W].bitcast(F32)
        tb = qt[:, 2 * W:4 * W].bitcast(F32)
        tc_ = qt[:, 4 * W:6 * W].bitcast(F32)
        bp0, btb, btcb = Buf("sp0"), Buf("stb"), Buf("stc")
        self.qt_take([bp0, btb, btcb])
        pso = self.TMP[0:15, 0:2048]
        bpso = Buf("spso")
        first_pso = True
        for c in range(8):
            g = c // 2
            w = 2 << g
            lid, j = ids[("p", c)]
            sl, bsl = self.acquire(lid)
            pb = cnt % 2
            cnt += 1
            proj(sl[:, j * 2048:(j + 1) * 2048], bsl, pb)
            for sq_ in range(2):
                tk.op(tk.dve, lambda: nc.vector.tensor_copy(out=p0[:, sq_ * 48:sq_ * 48 + 16], in_=self.SPT[:, (sq_ * 8 + c) * 16:(sq_ * 8 + c + 1) * 16]),
                      reads=[bspt], writes=[bp0])
                tk.op(tk.act, lambda: nc.scalar.activation(out=p0[:, sq_ * 48 + 16:sq_ * 48 + 48], in_=self.PS[pb][:, sq_ * 32:(sq_ + 1) * 32], func=AF.Copy),
                      reads=[self.bPS[pb]], writes=[bp0])
            src, bsrc = p0, bp0
            step = 1
            k = 0
            while step < w:
                dst, bdst = (tb, btb) if k % 2 == 0 else (tc_, btcb)
                lo = 2 * step
                tk.op(tk.dve, lambda: nc.vector.tensor_tensor(out=dst[:, lo:W], in0=src[:, lo:W], in1=src[:, lo - step:W - step], op=ALU.add),
                      reads=[bsrc], writes=[bdst])
                src, bsrc = dst, bdst
                step *= 2
                k += 1
            for sq_ in range(2):
                tk.op(tk.dve, lambda: nc.vector.scalar_tensor_tensor(out=dbs[:, c * NS + sq_ * 32:c * NS + (sq_ + 1) * 32], in0=src[:, sq_ * 48 + 16:sq_ * 48 + 48],
                                                                     scalar=1.0 / w, in1=p0[:, sq_ * 48 + 16:sq_ * 48 + 48], op0=ALU.mult, op1=ALU.subtract),
                      reads=[bsrc, bp0], writes=[bdbs])
                pbk = 2 + sq_ * 2 + c // 4
                tk.op(tk.pe, lambda: nc.tensor.transpose(out=self.PS[pbk][0:15, (c % 4) * 128:(c % 4 + 1) * 128], in_=p0[:, sq_ * 48 + 33:sq_ * 48 + 48],
                                                         identity=self.identf), reads=[bp0, self.bC], writes=[self.bPS[pbk]])
        self.tmp_take([bpso])
        for sq_ in range(2):
            for hf in range(2):
                tk.op(tk.act, lambda: nc.scalar.activation(out=pso[:, hf * 512:(hf + 1) * 512] if sq_ == 0 else pso[:, 1024 + hf * 512:1024 + (hf + 1) * 512],
                                                           in_=self.PS[2 + sq_ * 2 + hf][0:15, :], func=AF.Copy), reads=[self.bPS[2 + sq_ * 2 + hf]], writes=[bpso])
            tk.dma(tk.pool, self.ps_out[sq_ * 15:(sq_ + 1) * 15, :], pso[:, sq_ * 1024:(sq_ + 1) * 1024], reads=[bpso], writes=[self.b_out], dbuf=bpso)
        self.qt_take(self.bQT)
        for h in range(H):
            lid, j = ids[("q", h)]
            sl, bsl = self.acquire(lid)
            pb = cnt % 2
            cnt += 1
            proj(sl[:, j * 2048:(j + 1) * 2048], bsl, pb)
            tk.op(tk.act, lambda: nc.scalar.activation(out=self.QT[:, h * T:h * T + NS], in_=self.PS[pb][:, 0:NS], func=AF.Copy),
                  reads=[self.bPS[pb]], writes=[self.bQT[h]])
        aids = plan["attn"]
        kvb = [self.TMP[:, 1024 + i * 1024:1024 + (i + 1) * 1024].bitcast(BF16) for i in range(2)]
        bkvb = [Buf("skvb0"), Buf("skvb1")]
        rl = self.TMP[:, 0:64]
        brl = Buf("srl")
        self.tmp_take(bkvb + [brl])
        acnt = 0
        for h in range(H):
            started = False
            for sq_ in range(2):
                sl, bsl = self.acquire(aids[(h, sq_)])
                kv = kvb[acnt % 2]
                bkv = bkvb[acnt % 2]
                acnt += 1
                tk.op(tk.act, lambda: nc.scalar.activation(out=kv[:, 0:1024], in_=sl[:, 0:2048].bitcast(F32), func=AF.Copy), reads=[bsl], writes=[bkv])
                tk.op(tk.pool, lambda: nc.gpsimd.tensor_copy(out=kv[:, 1024:2048], in_=sl[:, 2048:4096].bitcast(F32)), reads=[bsl], writes=[bkv])
                qs = slice(sq_ * 32, (sq_ + 1) * 32)
                for kb in range(NB):
                    ps = cnt % 2
                    pt = cnt % 3
                    cnt += 1
                    tk.op(tk.pe, lambda: nc.tensor.matmul(self.PS[ps][:, 0:32], lhsT=kv[:, kb * 128:(kb + 1) * 128],
                                                          rhs=self.QT[:, h * T + sq_ * 32:h * T + (sq_ + 1) * 32], start=True, stop=True),
                          reads=[bkv, self.bQT[h]], writes=[self.bPS[ps]])
                    ptile = self.PT[:, pt * 512:pt * 512 + 32]
                    o_ = (sq_ * 8 + kb) * 8 + h
                    tk.op(tk.act, lambda: nc.scalar.activation(out=ptile, in_=self.PS[ps][:, 0:32], func=AF.Exp, scale=SCALE, bias=btc[:, o_:o_ + 1]),
                          reads=[self.bPS[ps], bsm], writes=[self.bPT[pt]])
                    tk.op(tk.pe, lambda: nc.tensor.matmul(self.PS[5][:, qs], lhsT=self.ONESB[:], rhs=ptile, start=(not started), stop=False),
                          reads=[self.bC, self.bPT[pt]], writes=[self.bPS[5]], signal=False)
                    tk.op(tk.pe, lambda: nc.tensor.matmul(self.PS[4][:, qs], lhsT=kv[:, 1024 + kb * 128:1024 + (kb + 1) * 128], rhs=ptile,
                                                          start=(not started), stop=False), reads=[bkv, self.bPT[pt]], writes=[self.bPS[4]])
                    started = True
            ps = cnt % 2
            pt = cnt % 3
            cnt += 1
            tk.op(tk.pe, lambda: nc.tensor.matmul(self.PS[ps][0:NS, 0:NS], lhsT=kn[:, h * NS:(h + 1) * NS], rhs=self.QT[:, h * T:h * T + NS],
                                                  start=True, stop=True), reads=[bkn, self.bQT[h]], writes=[self.bPS[ps]])
            ptile = self.PT[0:NS, pt * 512:pt * 512 + NS]
            tk.op(tk.act, lambda: nc.scalar.activation(out=ptile, in_=self.PS[ps][0:NS, 0:NS], func=AF.Exp, scale=SCALE, bias=btn[:, h:h + 1]),
                  reads=[self.bPS[ps], bsm], writes=[self.bPT[pt]])
            tk.op(tk.pool, lambda: nc.gpsimd.tensor_tensor(out=ptile, in0=ptile, in1=self.M64[0:NS, :], op=ALU.mult),
                  reads=[self.bPT[pt], bm64], writes=[self.bPT[pt]])
            tk.op(tk.pe, lambda: nc.tensor.matmul(self.PS[5][:, 0:NS], lhsT=self.ONESB[0:NS, :], rhs=ptile, start=False, stop=True),
                  reads=[self.bC, self.bPT[pt]], writes=[self.bPS[5]], signal=False)
            tk.op(tk.pe, lambda: nc.tensor.matmul(self.PS[4][:, 0:NS], lhsT=vn[0:NS, h * 128:(h + 1) * 128], rhs=ptile, start=False, stop=True),
                  reads=[bvn, self.bPT[pt]], writes=[self.bPS[4]])
            tk.op(tk.dve, lambda: nc.vector.reciprocal(out=rl, in_=self.PS[5][:, 0:NS]), reads=[self.bPS[5]], writes=[brl])
            tk.op(tk.dve, lambda: nc.vector.tensor_tensor(out=self.XT[:, h * NS:(h + 1) * NS], in0=self.PS[4][:, 0:NS], in1=rl, op=ALU.mult),
                  reads=[self.bPS[4], brl], writes=[self.bXT])
        self.bDB = bdbs
        self.mix(plan["mix"], ncol=NS, db=dbs)
        self.norm_to_xt(2, nblk=1, ntok=NS, npart=NS)
        self.ffn(plan["ffn2"], ncol=NS)
        self.final_out(lambda b: self.ys_out, nblk=1, npart=NS)


    def build(self):
        nc, tk = self.nc, self.tk
        self.tmp_cur = []
        self.hm_cur = self.bHM
        self.qt_cur = self.bQT
        self.prologue()
        self.load_x(0)
        self.convert(0)
        splan = None
        plan = []
        for s in range(self.nslot):
            full = (s % 2 == 1)
            if s == min(2, self.nslot - 1) and self.with_sample:
                splan = {"ffn1": self.ffn_loads(0), "inproj": self.inproj_loads(True), "attn": self.sample_attn_loads(),
                         "mix": self.mix_loads(), "ffn2": self.ffn_loads(1)}
            ent = {"ffn1": self.ffn_loads(0), "inproj": self.inproj_loads(full)}
            if full:
                ent["attn"] = self.attn_loads(s)
                ent["mix"] = self.mix_loads()
                ent["ffn2"] = self.ffn_loads(1)
            plan.append(ent)
        import os
        stop = int(os.environ.get("KSTOP", "99"))
        for s in range(self.nslot):
            full = (s % 2 == 1)
            last_full = (s == self.nslot - 1)
            ent = plan[s]
            if stop == 0:
                break
            if s == min(2, self.nslot - 1) and self.with_sample:
                self.sample_slot(splan)
            if s > 0:
                self.load_x(s)
            self.norm_to_xt(0)
            if stop == 1:
                break
            self.ffn(ent["ffn1"])
            if s == 0:
                self.convert(1)
            if stop == 2:
                break
            self.norm_to_xt(1)
            self.inproj(s, ent["inproj"], full, last_full)
            if stop == 3:
                break
            if full:
                self.attention(s, ent["attn"])
                self.mix(ent["mix"])
                self.norm_to_xt(2)
                self.ffn(ent["ffn2"])
                i = s // 2
                self.final_out(lambda b, i=i: self.y_out[i * T + b * 128: i * T + (b + 1) * 128, :])
        for E in (tk.pe, tk.act, tk.dve, tk.pool):
            for E2 in (tk.pe, tk.act, tk.dve, tk.pool):
                if E2 is not E and E2.count > 0:
                    tk._wait(E, {E2.name: (E2.sem, E2.count)})
        tk.finish(tk.sp)
        tk.finish(tk.pool)
        return nc


_CACHE = {}


def _consts(core):
    g = core % 2
    identf = np.eye(128, dtype=np.float32)
    identb = np.eye(128).astype(ml_dtypes.bfloat16)
    k = np.arange(128)[:, None]
    q = np.arange(128)[None, :]
    trib = (k <= q).astype(np.float32).astype(ml_dtypes.bfloat16)
    onesf = np.ones((128, 128), np.float32)
    kmask = np.zeros((128, 64), np.float32)
    if g == 0:
        kmask[:, 0:8] = NEG
    diag8 = np.zeros((8, 8, 8), np.float32)
    for h in range(8):
        diag8[h, h, :] = 1.0
    poolfix = np.ones((128, 4, 16), np.float32)
    if g == 0:
        for gi, w in enumerate((2, 4, 8, 16)):
            pos = np.arange(16)
            cnt = np.minimum(pos + 1, w)
            poolfix[:, gi, :] = (w / cnt)[None, :]
    selb = np.zeros((2, 8, 8, 128), np.float32)
    for h in range(8):
        selb[:, h, h, :] = 1.0
    return dict(selb=selb.reshape(16, 1024).astype(ml_dtypes.bfloat16),
                identf=identf, identb=identb, trib=trib, onesf=onesf, kmask=kmask, diag8=diag8.reshape(8, 64),
                poolfix=poolfix.reshape(128, 64))


def kernel(x_prompt, x_sample, cache_k, cache_v, cache_logf, state_pool,
           g_ffn1, w1_gate, w1_up, w1_down, g_mix, w_in, b_f, w_pool, pool_scale,
           w_o, g_ffn2, w2_gate, w2_up, w2_down, g_final, _npair=4, _cores=8):
    npair = _npair
    ns = 2 * npair
    key = (npair,)
    if key not in _CACHE:
        _CACHE[key] = Prog(npair, True).build()
    nc = _CACHE[key]
    f32 = lambda a: np.ascontiguousarray(np.asarray(a, dtype=np.float32))
    shared = {
        "w1_gate": f32(w1_gate[0]), "w1_up": f32(w1_up[0]), "w1_down": f32(w1_down[0]),
        "w2_gate": f32(w2_gate[0]), "w2_up": f32(w2_up[0]), "w2_down": f32(w2_down[0]),
        "w_in": f32(w_in[0]),
        "w_f": f32(np.asarray(w_in[0])[:, 3072:3080].reshape(KC, 128, 8).transpose(1, 0, 2).reshape(128, KC * 8)), "w_o": f32(w_o[0]), "w_pool": f32(np.asarray(w_pool[0]).reshape(1024, 256)),
        "gcols": f32(np.concatenate([np.asarray(g).reshape(KC, 128).T for g in (g_ffn1[0], g_mix[0], g_ffn2[0])], axis=1)),
        "gfin": f32(np.broadcast_to(np.asarray(g_final)[None, :], (128, D))),
        "bfcol": f32(np.asarray(b_f[0]).reshape(8, 1)),
        "pscale": f32(np.asarray(pool_scale[0]).reshape(8, 128).T),
    }
    xp = np.asarray(x_prompt)
    in_maps = []
    for c in range(_cores):
        b, g = c // 2, c % 2
        xs = np.zeros((ns * T, D), np.float32)
        for s in range(ns):
            t = s if g == 1 else s - 1
            if t >= 0:
                xs[s * T:(s + 1) * T] = xp[b, t * T:(t + 1) * T]
        m = dict(shared)
        m["xs"] = xs
        m.update(_consts(c))
        sq = slice(2 * c, 2 * c + 2)
        m["xsmp"] = f32(np.asarray(x_sample)[sq].reshape(64, D))
        ck = np.asarray(cache_k)[0, sq]
        m["ckT"] = f32(ck.transpose(0, 2, 3, 1).reshape(2 * H * 128, 1024))
        cvv = np.asarray(cache_v)[0, sq].reshape(2, NB, 128, H, HD)
        m["cv"] = f32(cvv.transpose(0, 3, 2, 1, 4).reshape(2 * H * 128, NB * HD))
        m["clfT"] = f32(np.asarray(cache_logf)[0, sq].transpose(0, 2, 1).reshape(2 * H, 1024))
        sp = np.asarray(state_pool)[0, sq]
        spt = np.zeros((128, 2, 8, 16), np.float32)
        spt[:, :, :, 1:16] = sp.reshape(2, 15, 8, 128).transpose(3, 0, 2, 1)
        m["spT"] = spt.reshape(128, 256)
        kq = np.arange(64)
        m["mask64"] = ((kq[:, None] // 32 == kq[None, :] // 32) & (kq[:, None] <= kq[None, :])).astype(np.float32).astype(ml_dtypes.bfloat16)
        in_maps.append(m)
    res = run_bass_kernel_spmd(nc, in_maps, core_ids=list(range(_cores)))
    R = res.results
    B_ = _cores // 2
    S = ns * T if False else None
    ntile = ns
    y = np.zeros((B_, ntile * T, D), np.float32)
    kk = np.zeros((1, B_, ntile * T, H, HD), np.float32)
    vv = np.zeros((1, B_, ntile * T, H, HD), np.float32)
    lf = np.zeros((1, B_, ntile * T, H), np.float32)
    pp = np.zeros((1, B_, 15, 1024), np.float32)
    for c in range(_cores):
        b, g = c // 2, c % 2
        r = R[c]
        for i in range(npair):
            t = 2 * i + g
            y[b, t * T:(t + 1) * T] = r["y_out"][i * T:(i + 1) * T]
        if g == 1:
            kk[0, b] = r["k_out"].reshape(ntile * T, H, HD)
            vv[0, b] = r["v_out"].reshape(ntile * T, H, HD)
            lf[0, b] = r["lf_out"].reshape(ns, 128, NB, H).transpose(0, 2, 1, 3).reshape(ntile * T, H)
            pp[0, b] = r["pool_out"]
    DB = 16
    ys = np.zeros((DB, 32, D), np.float32)
    ks_ = np.zeros((1, DB, 32, H, HD), np.float32)
    vs_ = np.zeros((1, DB, 32, H, HD), np.float32)
    lfs = np.zeros((1, DB, 32, H), np.float32)
    pps = np.zeros((1, DB, 15, 1024), np.float32)
    for c in range(_cores):
        r = R[c]
        sq = slice(2 * c, 2 * c + 2)
        ys[sq] = r["ys_out"].reshape(2, 32, D)
        ks_[0, sq] = r["ks_out"].reshape(2, 32, H, HD)
        vs_[0, sq] = r["vs_out"].reshape(2, 32, H, HD)
        lfs[0, sq] = r["lfs_out"].reshape(2, 32, H)
        pps[0, sq] = r["ps_out"].reshape(2, 15, 1024)
    if _npair != 4 or _cores != 8:
        return y, kk, vv, lf, pp, ys, ks_, vs_, lfs, pps
    return y, ys, kk, vv, lf, pp, ks_, vs_, lfs, pps
```

```python
import numpy as np
import ml_dtypes
import concourse.bass as bass
import concourse.mybir as mybir
from concourse.bass_utils import run_bass_kernel_spmd

F32 = mybir.dt.float32
BF16 = mybir.dt.bfloat16
AF = mybir.ActivationFunctionType
ALU = mybir.AluOpType

D = 2048
DFF = 5632
NFC = DFF // 128
KC = D // 128
T = 1024
NB = T // 128
H = 8
HD = 128
INW = 4104
EPS = 1e-6
SCALE = HD ** -0.5
NEG = -30000.0
SLOT_B = 8192
NRING = 5
PREFETCH = 3


class Buf:
    __slots__ = ("name", "w", "r")

    def __init__(self, name):
        self.name = name
        self.w = {}
        self.r = {}


class Eng:
    def __init__(self, nc, name, eng, inorder_safe=False):
        self.name = name
        self.eng = eng
        self.sem = nc.alloc_semaphore("sem_" + name)
        self.count = 0
        self.waited = {}
        self.inorder_safe = inorder_safe


class Trk:
    def __init__(self, nc):
        self.nc = nc
        self.pe = Eng(nc, "pe", nc.tensor, True)
        self.act = Eng(nc, "act", nc.scalar)
        self.dve = Eng(nc, "dve", nc.vector)
        self.pool = Eng(nc, "pool", nc.gpsimd)
        self.sp = Eng(nc, "sp", nc.sync, True)
        self.dma_sems = {}

    def _wait(self, E, evs, self_ok=False):
        for key, (sem, val) in evs.items():
            if sem is E.sem and (E.inorder_safe or self_ok):
                continue
            if E.waited.get(key, 0) >= val:
                continue
            E.eng.wait_ge(sem, val)
            E.waited[key] = val

    @staticmethod
    def _deps(reads, writes):
        evs = {}

        def add(d):
            for k, (s, v) in d.items():
                if k not in evs or evs[k][1] < v:
                    evs[k] = (s, v)
        for b in reads:
            add(b.w)
        for b in writes:
            add(b.w)
            add(b.r)
        return evs

    @staticmethod
    def _record(key, ev, reads, writes):
        for b in reads:
            if key not in b.r or b.r[key][1] < ev[1]:
                b.r[key] = ev
        for b in writes:
            b.w = {key: ev}
            b.r = {}

    def op(self, E, fn, reads=(), writes=(), signal=True, self_ok=False):
        self._wait(E, self._deps(reads, writes), self_ok)
        inst = fn()
        if signal:
            E.count += 1
            inst.then_inc(E.sem, 1)
            ev = (E.sem, E.count)
        else:
            ev = (E.sem, E.count + 1)
        self._record(E.name, ev, reads, writes)
        return inst

    def dma(self, Q, out, in_, reads=(), writes=(), dbuf=None, slow=False):
        if dbuf is None:
            dbuf = writes[0] if writes else reads[0]
        key = "dma_" + dbuf.name
        deps = self._deps(reads, writes)
        deps.pop(key, None)
        self._wait(Q, deps)
        if key not in self.dma_sems:
            self.dma_sems[key] = [self.nc.alloc_semaphore(key), 0]
        ent = self.dma_sems[key]
        ent[1] += 16
        if slow:
            with self.nc.allow_non_contiguous_dma(reason="small strided transfer"):
                Q.eng.dma_start(out=out, in_=in_).then_inc(ent[0], 16)
        else:
            Q.eng.dma_start(out=out, in_=in_).then_inc(ent[0], 16)
        self._record(key, (ent[0], ent[1]), reads, writes)

    @staticmethod
    def handoff(old, new):
        evs = {}
        for b in old:
            for d in (b.w, b.r):
                for k, (s, v) in d.items():
                    if k not in evs or evs[k][1] < v:
                        evs[k] = (s, v)
        for b in new:
            b.w = {}
            b.r = dict(evs)

    def finish(self, E):
        for key, (sem, val) in self.dma_sems.items():
            if E.waited.get(key, 0) < val:
                E.eng.wait_ge(sem, val)
                E.waited[key] = val


class Prog:
    def __init__(self, npair, with_sample):
        self.npair = npair
        self.nslot = 2 * npair
        self.with_sample = with_sample
        nc = self.nc = bass.Bass("TRN2", target_bir_lowering=False)
        self.tk = Trk(nc)
        self._dram()
        self._sbuf()
        self.loads = []
        self.issued = 0
        self.next_load = 0

    def _dram(self):
        nc = self.nc
        ns = self.nslot
        di = lambda n, s, dt=F32: nc.dram_tensor(n, list(s), dt, kind="ExternalInput").ap()
        do = lambda n, s, dt=F32: nc.dram_tensor(n, list(s), dt, kind="ExternalOutput").ap()
        dn = lambda n, s, dt=BF16: nc.dram_tensor(n, list(s), dt).ap()
        self.xs = di("xs", (ns * T, D))
        self.w_gate = [di("w1_gate", (D, DFF)), di("w2_gate", (D, DFF))]
        self.w_up = [di("w1_up", (D, DFF)), di("w2_up", (D, DFF))]
        self.w_down = [di("w1_down", (DFF, D)), di("w2_down", (DFF, D))]
        self.w_in = di("w_in", (D, INW))
        self.w_o = di("w_o", (D, D))
        self.wf_d = di("w_f", (128, KC * 8))
        self.w_pool = di("w_pool", (4 * 256, 256))
        self.gcols = di("gcols", (128, 3 * KC))
        self.gfin = di("gfin", (128, D))
        self.bfcol = di("bfcol", (8, 1))
        self.pscale = di("pscale", (128, 8))
        self.identb_d = di("identb", (128, 128), BF16)
        self.identf_d = di("identf", (128, 128))
        self.trib_d = di("trib", (128, 128), BF16)
        self.onesf_d = di("onesf", (128, 128))
        self.kmask_d = di("kmask", (128, 64))
        self.diag8_d = di("diag8", (8, 64))
        self.selb_d = di("selb", (16, 1024), BF16)
        self.poolfix_d = di("poolfix", (128, 64))
        self.xsmp = di("xsmp", (64, D))
        self.ckT = di("ckT", (2 * H * 128, 1024))
        self.cv = di("cv", (2 * H * 128, 1024))
        self.clfT = di("clfT", (2 * H, 1024))
        self.spT = di("spT", (128, 256))
        self.mask64_d = di("mask64", (64, 64), BF16)
        self.ys_out = do("ys_out", (64, D))
        self.ks_out = do("ks_out", (64, H * HD))
        self.vs_out = do("vs_out", (64, H * HD))
        self.lfs_out = do("lfs_out", (64, H))
        self.ps_out = do("ps_out", (30, 1024))
        self.y_out = do("y_out", (self.npair * T, D))
        self.k_out = do("k_out", (ns * T, H * HD))
        self.v_out = do("v_out", (ns * T, H * HD))
        self.lf_out = do("lf_out", (ns * 128, NB * H))
        self.pool_out = do("pool_out", (15, 1024))
        self.wg_s = [dn("wg1_s", (D, DFF)), dn("wg2_s", (D, DFF))]
        self.wu_s = [dn("wu1_s", (D, DFF)), dn("wu2_s", (D, DFF))]
        self.wd_s = [dn("wd1_s", (DFF, D)), dn("wd2_s", (DFF, D))]
        self.win_s = dn("win_s", (D, INW))
        self.wo_s = dn("wo_s", (D, D))
        self.wp_s = dn("wp_s", (4 * 256, 256))
        self.kt_hist = dn("kt_hist", (ns * H * 128, T))
        self.v_hist = dn("v_hist", (ns * H * 128, NB * HD))
        B = Buf
        self.b_wg = [[B("wg%d_%d" % (f, q)) for q in range(4)] for f in range(2)]
        self.b_wu = [[B("wu%d_%d" % (f, q)) for q in range(4)] for f in range(2)]
        self.b_wd = [[B("wd%d_%d" % (f, q)) for q in range(4)] for f in range(2)]
        self.b_win = B("win")
        self.b_wo = B("wo")
        self.b_wp = B("wp")
        self.b_kth = [[B("kth%d_%d" % (s, h)) for h in range(H)] for s in range(ns)]
        self.b_vh = [[B("vh%d_%d" % (s, h)) for h in range(H)] for s in range(ns)]
        self.b_out = B("outs")

    def _sbuf(self):
        nc = self.nc
        sb = lambda n, cols, dt: nc.alloc_sbuf_tensor(n, [128, cols], dt)
        self.R = sb("R", NB * D, F32)
        self.bR = [[Buf("R%d_%d" % (b, q)) for q in range(4)] for b in range(NB)]
        self.bRld = [Buf("Rld%d" % b) for b in range(NB)]
        self.bRst = [Buf("Rst%d" % b) for b in range(NB)]
        self.XT = sb("XT", KC * T, BF16)
        self.bXT = Buf("XT")
        self.HM = sb("HM", 2 * 4 * T, BF16)
        self.bHM = [Buf("HM0"), Buf("HM1")]
        self.RING = sb("RING", NRING * SLOT_B // 2, BF16)
        self.bRING = [Buf("ring%d" % i) for i in range(NRING)]
        self.QT = sb("QT", H * T, BF16)
        self.bQT = [Buf("QT%d" % h) for h in range(H)]
        self.TMP = sb("TMP", 12288 // 4, F32)
        self.GFIN = sb("GFIN", D, F32)
        self.PT = sb("PT", 3 * 512, BF16)
        self.bPT = [Buf("PT%d" % i) for i in range(3)]
        self.ONESB = sb("ONESB", 128, BF16)
        self.HS = sb("HS", T, BF16)
        self.bHS = Buf("HS")
        self.SELB = sb("SELB", 1024, BF16)
        self.BTK = sb("BTK", 128, F32)
        self.bBTK = [Buf("BTK0"), Buf("BTK1")]
        self.CK = sb("CK", H * 64, F32)
        self.bCK = Buf("CK")
        self.CR = sb("CR", H * NB, F32)
        self.bCR = Buf("CR")
        self.BT = sb("BT", 4 * NB, F32)
        self.bBT = [Buf("BT%d" % i) for i in range(4)]
        self.CONST = sb("CONST", 704, F32)
        self.bC = Buf("const")
        c = self.CONST
        self.identf = c[:, 0:128]
        self.onesf = c[:, 128:256]
        self.identb = c[:, 256:320].bitcast(BF16)
        self.trib = c[:, 320:384].bitcast(BF16)
        self.pad0 = c[:, 384:448]
        self.gc = c[:, 448:496]
        self.psc = c[:, 496:504]
        self.kmask = c[:, 504:568]
        self.poolfix = c[:, 568:632]
        self.diag8 = c[0:8, 632:696]
        self.bfc = c[0:8, 696:697]
        self.nbfc = c[0:8, 697:698]
        self.cprev = c[0:8, 698:699]
        self.onesrow = c[0:8, 700:704]
        self.SMP = sb("SMP", 512, F32)
        self.SPT = sb("SPT", 256, F32)
        self.M64 = sb("M64", 64, BF16)
        self.ST = sb("ST", 32, F32)
        self.bST = Buf("ST")
        self.HALO = sb("HALO", 8 * 16, F32)
        self.bHALO = Buf("HALO")
        self.PS = [nc.alloc_psum_tensor("ps%d" % i, [128, 512], F32) for i in range(7)]
        self.PSB = nc.alloc_psum_tensor("psb", [128, 1024], BF16)
        self.bPS = [Buf("ps%d" % i) for i in range(8)]

    def tmp_take(self, new):
        Trk.handoff(self.tmp_cur, new)
        self.tmp_cur = list(new)

    def hm_take(self, new):
        if new is self.hm_cur:
            return
        Trk.handoff(self.hm_cur, new)
        self.hm_cur = new

    def qt_take(self, new):
        if new is self.qt_cur:
            return
        Trk.handoff(self.qt_cur, new)
        self.qt_cur = new

    def slot_ap(self, i):
        return self.RING[:, i * (SLOT_B // 2):(i + 1) * (SLOT_B // 2)]

    def add_load(self, fn):
        self.loads.append(fn)
        return len(self.loads) - 1

    def acquire(self, j):
        lim = min(len(self.loads), j + 1 + PREFETCH)
        while self.issued < lim:
            i = self.issued
            s = i % NRING
            self.loads[i](self.slot_ap(s), self.bRING[s])
            self.issued += 1
        s = j % NRING
        return self.slot_ap(s), self.bRING[s]

    def prologue(self):
        nc, tk = self.nc, self.tk
        c = self.CONST
        cd = lambda dst, src: tk.dma(tk.sp, dst, src, writes=[self.bC])
        cd(self.identf, self.identf_d)
        cd(self.onesf, self.onesf_d)
        cd(self.identb, self.identb_d)
        cd(self.trib, self.trib_d)
        cd(self.gc, self.gcols)
        cd(self.psc, self.pscale)
        cd(self.kmask, self.kmask_d)
        cd(self.poolfix, self.poolfix_d)
        cd(self.diag8, self.diag8_d)
        cd(self.bfc, self.bfcol)
        tk.op(tk.dve, lambda: nc.vector.memset(self.SELB[:], 0.0), writes=[self.bC])
        tk.op(tk.dve, lambda: nc.vector.memset(self.HS[:], 0.0), writes=[self.bHS])
        cd(self.SELB[0:16, :], self.selb_d)
        tk.dma(tk.sp, self.GFIN[:], self.gfin, writes=[self.bC])
        tk.op(tk.dve, lambda: nc.vector.tensor_scalar(out=self.nbfc, in0=self.bfc, scalar1=-1.0, scalar2=None, op0=ALU.mult),
              reads=[self.bC], writes=[self.bC])
        tk.op(tk.dve, lambda: nc.vector.memset(self.cprev, 0.0), writes=[self.bC])
        tk.op(tk.dve, lambda: nc.vector.memset(self.HALO[:], 0.0), writes=[self.bHALO])
        tk.op(tk.dve, lambda: nc.vector.memset(self.pad0, 0.0), writes=[self.bC])
        tk.op(tk.dve, lambda: nc.vector.memset(self.ONESB[:], 1.0), writes=[self.bC])
        tk.op(tk.dve, lambda: nc.vector.memset(self.CK[:], 0.0), writes=[self.bCK])

    def convert(self, part):
        tk = self.tk

        def conv(dst, src, buf, rows_per):
            n = src.shape[0]
            for r in range(0, n, rows_per):
                tk.dma(tk.pool, dst[r:r + rows_per, :], src[r:r + rows_per, :], writes=[buf], dbuf=buf)
        QC = DFF // 4

        def convq(f, q):
            cs = slice(q * QC, (q + 1) * QC)
            for r in range(0, D, 512):
                tk.dma(tk.pool, self.wg_s[f][r:r + 512, cs], self.w_gate[f][r:r + 512, cs], writes=[self.b_wg[f][q]], dbuf=self.b_wg[f][q])
            for r in range(0, D, 512):
                tk.dma(tk.pool, self.wu_s[f][r:r + 512, cs], self.w_up[f][r:r + 512, cs], writes=[self.b_wu[f][q]], dbuf=self.b_wu[f][q])
            for r in range(q * QC, (q + 1) * QC, 704):
                tk.dma(tk.pool, self.wd_s[f][r:r + 704, :], self.w_down[f][r:r + 704, :], writes=[self.b_wd[f][q]], dbuf=self.b_wd[f][q])
        for f in ([0] if part == 0 else [1]):
            if f == 1:
                conv(self.wp_s, self.w_pool, self.b_wp, 1024)
                conv(self.wo_s, self.w_o, self.b_wo, 512)
            for q in range(4):
                convq(f, q)
            if f == 0:
                conv(self.win_s, self.w_in, self.b_win, 256)

    def load_x(self, slot, nblk=NB):
        tk = self.tk
        for b in range(nblk):
            r0 = slot * T + b * 128
            tk.dma(tk.sp, self.R[:, b * D:(b + 1) * D], self.xs[r0:r0 + 128, :], writes=self.bR[b], dbuf=self.bRld[b])

    def norm_to_xt(self, gidx, nblk=NB, ntok=T, npart=128):
        nc, tk = self.nc, self.tk
        sq = self.TMP[:, 0:1024].bitcast(BF16)
        xnb = self.TMP[:, 1024:2048].bitcast(BF16)
        bsq, bxn = Buf("sq"), Buf("xnb")
        self.tmp_take([bsq, bxn])
        P_ = npart
        sq = sq[0:P_, :]
        xnb = xnb[0:P_, :]
        for b in range(nblk):
            Rb = self.R[0:P_, b * D:(b + 1) * D]
            st = self.ST[0:P_, :]
            tk.op(tk.act, lambda: nc.scalar.activation(out=sq, in_=Rb, func=AF.Square, accum_out=st[:, 3 * b:3 * b + 1]),
                  reads=self.bR[b], writes=[bsq, self.bST])
            tk.op(tk.act, lambda: nc.scalar.activation(out=st[:, 3 * b + 1:3 * b + 2], in_=st[:, 3 * b:3 * b + 1], func=AF.Sqrt,
                                                       scale=1.0 / D, bias=EPS), reads=[self.bST], writes=[self.bST])
            tk.op(tk.dve, lambda: nc.vector.reciprocal(out=st[:, 3 * b + 2:3 * b + 3], in_=st[:, 3 * b + 1:3 * b + 2]),
                  reads=[self.bST], writes=[self.bST])
            tk.op(tk.act, lambda: nc.scalar.activation(out=xnb, in_=Rb, func=AF.Copy, scale=st[:, 3 * b + 2:3 * b + 3]),
                  reads=self.bR[b] + [self.bST], writes=[bxn])
            for half in range(2):
                for j in range(8):
                    kc = half * 8 + j
                    tk.op(tk.pe, lambda: nc.tensor.transpose(out=self.PSB[:, j * 128:j * 128 + P_], in_=xnb[:, kc * 128:(kc + 1) * 128],
                                                             identity=self.identb[0:P_, 0:P_]), reads=[bxn, self.bC], writes=[self.bPS[7]])
                for j in range(8):
                    kc = half * 8 + j
                    tk.op(tk.dve, lambda: nc.vector.tensor_scalar(out=self.XT[:, kc * ntok + b * 128: kc * ntok + b * 128 + P_],
                                                                  in0=self.PSB[:, j * 128:j * 128 + P_],
                                                                  scalar1=self.gc[:, gidx * KC + kc: gidx * KC + kc + 1], scalar2=None,
                                                                  op0=ALU.mult), reads=[self.bPS[7], self.bC], writes=[self.bXT], self_ok=True)

    def ffn_loads(self, f):
        tk = self.tk
        ids = {}
        for grp in range(NFC // 4):
            for j in range(4):
                fc = grp * 4 + j

                def ld(slot, buf, fc=fc):
                    cs = slice(fc * 128, (fc + 1) * 128)
                    tk.dma(tk.sp, slot[:, 0:2048].rearrange("p (k c) -> p k c", c=128),
                           self.wg_s[f][:, cs].rearrange("(k p) c -> p k c", p=128), reads=[self.b_wg[f][fc // 11]], writes=[buf])
                    tk.dma(tk.sp, slot[:, 2048:4096].rearrange("p (k c) -> p k c", c=128),
                           self.wu_s[f][:, cs].rearrange("(k p) c -> p k c", p=128), reads=[self.b_wu[f][fc // 11]], writes=[buf])
                ids[("gu", fc)] = self.add_load(ld)
            if grp >= 1:
                self._ffn_dloads(f, grp - 1, ids)
        self._ffn_dloads(f, NFC // 4 - 1, ids)
        return ids

    def _ffn_dloads(self, f, grp, ids):
        tk = self.tk
        for half in range(2):
            def ld(slot, buf, half=half, grp=grp):
                r0 = (grp * 4 + half * 2) * 128
                tk.dma(tk.sp, slot[:, 0:4096].rearrange("p (j c) -> p j c", c=D),
                       self.wd_s[f][r0:r0 + 256, :].rearrange("(j p) c -> p j c", p=128), reads=[self.b_wd[f][(grp * 4 + half * 2) // 11], self.b_wd[f][(grp * 4 + half * 2 + 1) // 11]], writes=[buf])
            ids[("d", grp, half)] = self.add_load(ld)

    def ffn(self, ids, ncol=T):
        nc, tk = self.nc, self.tk
        sgs = [self.TMP[:, 0:512], self.TMP[:, 512:1024]]
        bsg = [Buf("sg0"), Buf("sg1")]
        self.tmp_take(bsg)
        self.hm_take(self.bHM)
        halves = [(c0, min(512, ncol - c0)) for c0 in range(0, ncol, 512)]
        nblk = (ncol + 127) // 128
        cnt = 0
        dcnt = 0

        def down(grp):
            nonlocal dcnt
            hb = grp % 2
            sa, ba = self.acquire(ids[("d", grp, 0)])
            sb_, bb = self.acquire(ids[("d", grp, 1)])
            for b in range(nblk):
                nt = min(128, ncol - b * 128)
                for slab in range(4):
                    pb = 4 + dcnt % 2
                    dcnt += 1
                    for j in range(4):
                        sl, bsl = (sa, ba) if j < 2 else (sb_, bb)
                        tk.op(tk.pe, lambda: nc.tensor.matmul(self.PS[pb][0:nt, :],
                                                              lhsT=self.HM[:, (hb * 4 + j) * T + b * 128:(hb * 4 + j) * T + b * 128 + nt],
                                                              rhs=sl[:, (j % 2) * D + slab * 512:(j % 2) * D + (slab + 1) * 512],
                                                              start=(j == 0), stop=(j == 3)),
                              reads=[self.bHM[hb], bsl], writes=[self.bPS[pb]], signal=(j == 3))
                    Rs = self.R[0:nt, b * D + slab * 512: b * D + (slab + 1) * 512]
                    tk.op(tk.dve, lambda: nc.vector.scalar_tensor_tensor(out=Rs, in0=self.PS[pb][0:nt, :], scalar=0.5, in1=Rs,
                                                                         op0=ALU.mult, op1=ALU.add),
                          reads=[self.bPS[pb], self.bR[b][slab]], writes=[self.bR[b][slab]])

        for grp in range(NFC // 4):
            hb = grp % 2
            for j in range(4):
                fc = grp * 4 + j
                sl, bsl = self.acquire(ids[("gu", fc)])
                for (c0, n) in halves:
                    pp = cnt % 2
                    cnt += 1
                    pg, pu = 2 * pp, 2 * pp + 1
                    for kc in range(KC):
                        tk.op(tk.pe, lambda: nc.tensor.matmul(self.PS[pg][:, 0:n], lhsT=sl[:, kc * 128:(kc + 1) * 128],
                                                              rhs=self.XT[:, kc * ncol + c0: kc * ncol + c0 + n],
                                                              start=(kc == 0), stop=(kc == KC - 1)),
                              reads=[bsl, self.bXT], writes=[self.bPS[pg]], signal=(kc == KC - 1))
                    for kc in range(KC):
                        tk.op(tk.pe, lambda: nc.tensor.matmul(self.PS[pu][:, 0:n], lhsT=sl[:, 2048 + kc * 128:2048 + (kc + 1) * 128],
                                                              rhs=self.XT[:, kc * ncol + c0: kc * ncol + c0 + n],
                                                              start=(kc == 0), stop=(kc == KC - 1)),
                              reads=[bsl, self.bXT], writes=[self.bPS[pu]], signal=(kc == KC - 1))
                    tk.op(tk.act, lambda: nc.scalar.activation(out=sgs[pp][:, 0:n], in_=self.PS[pg][:, 0:n], func=AF.Silu),
                          reads=[self.bPS[pg]], writes=[bsg[pp]])
                    tk.op(tk.dve, lambda: nc.vector.tensor_tensor(out=self.HM[:, (hb * 4 + j) * T + c0:(hb * 4 + j) * T + c0 + n],
                                                                  in0=sgs[pp][:, 0:n], in1=self.PS[pu][:, 0:n], op=ALU.mult),
                          reads=[bsg[pp], self.bPS[pu]], writes=[self.bHM[hb]], self_ok=True)
            if grp >= 1:
                down(grp - 1)
        down(NFC // 4 - 1)

    def inproj_loads(self, full):
        tk = self.tk
        chunks = [("k", h) for h in range(H)] + [("v", h) for h in range(H)] + [("p", c) for c in range(8)]
        if full:
            chunks += [("q", h) for h in range(H)]
        col0 = {"q": 0, "k": 1024, "v": 2048, "p": 3080}
        ids = {}
        for i in range(0, len(chunks), 2):
            pair = chunks[i:i + 2]

            def ld(slot, buf, pair=pair, first=(i == 0)):
                for j, (kind, idx) in enumerate(pair):
                    c = col0[kind] + idx * 128
                    tk.dma(tk.sp, slot[:, j * 2048:(j + 1) * 2048].rearrange("p (k c) -> p k c", c=128),
                           self.win_s[:, c:c + 128].rearrange("(k p) c -> p k c", p=128), reads=[self.b_win], writes=[buf])
            lid = self.add_load(ld)
            for j, ch in enumerate(pair):
                ids[ch] = (lid, j)
        return ids

    def load_wf(self):
        nc, tk = self.nc, self.tk
        self.WF = nc.alloc_sbuf_tensor("WF", [128, KC * 8], BF16)
        wf32 = nc.alloc_sbuf_tensor("WF32", [128, KC * 8], F32)
        b32 = Buf("wf32")
        tk.dma(tk.sp, wf32[:], self.wf_d, writes=[b32])
        tk.op(tk.dve, lambda: nc.vector.tensor_copy(out=self.WF[:], in_=wf32[:]), reads=[b32], writes=[self.bC])

    def inproj(self, slot, ids, full, last_full):
        nc, tk = self.nc, self.tk
        hm = self.HM
        kts = [hm[:, 0:1024], hm[:, 1024:2048]]
        kf = hm[:, 2048:4096].bitcast(F32)
        outk = [hm[:, 4096:6144].bitcast(F32), hm[:, 6144:8192].bitcast(F32)]
        bkts = [Buf("kts0"), Buf("kts1")]
        bkf = Buf("kf")
        boutk = [Buf("outk0"), Buf("outk1")]
        self.hm_take(bkts + [bkf] + boutk)
        if not getattr(self, "wf_loaded", False):
            self.load_wf()
            self.wf_loaded = True
        tmp = self.TMP
        kfs = [kf, tmp[:, 0:1024]]
        bkfs = [bkf, Buf("kf2")]
        self.tmp_take([bkfs[1]])
        cnt = 0

        def proj(w_ap, bw, c0, n, pbank, m=128, wcols=128):
            for kc in range(KC):
                tk.op(tk.pe, lambda: nc.tensor.matmul(self.PS[pbank][0:m, 0:n], lhsT=w_ap[:, kc * wcols: kc * wcols + m],
                                                      rhs=self.XT[:, kc * T + c0: kc * T + c0 + n],
                                                      start=(kc == 0), stop=(kc == KC - 1)),
                      reads=[bw, self.bXT], writes=[self.bPS[pbank]], signal=(kc == KC - 1))

        items = [("k", h) for h in range(H)] + [("v", h) for h in range(H)]

        def stage_a(i):
            nonlocal cnt
            kind, h = items[i]
            lid, j = ids[(kind, h)]
            sl, bsl = self.acquire(lid)
            w_ap = sl[:, j * 2048:(j + 1) * 2048]
            kb_ = i % 2
            for half in range(2):
                pb = cnt % 2
                cnt += 1
                proj(w_ap, bsl, half * 512, 512, pb)
                tk.op(tk.dve, lambda: nc.vector.tensor_copy(out=kfs[kb_][:, half * 512:(half + 1) * 512], in_=self.PS[pb][:, :]),
                      reads=[self.bPS[pb]], writes=[bkfs[kb_]], self_ok=True)
                if kind == "k":
                    tk.op(tk.act, lambda: nc.scalar.activation(out=kts[kb_][:, half * 512:(half + 1) * 512],
                                                               in_=kfs[kb_][:, half * 512:(half + 1) * 512], func=AF.Copy),
                          reads=[bkfs[kb_]], writes=[bkts[kb_]], self_ok=True)
            if kind == "k":
                r0 = (slot * H + h) * 128
                tk.dma(tk.pool, self.kt_hist[r0:r0 + 128, :], kts[kb_], reads=[bkts[kb_]], writes=[self.b_kth[slot][h]], dbuf=bkts[kb_])

        def stage_b(i):
            kind, h = items[i]
            kb_ = i % 2
            for q4 in range(2):
                pt = 2 + q4
                for jj in range(4):
                    b = q4 * 4 + jj
                    tk.op(tk.pe, lambda: nc.tensor.transpose(out=self.PS[pt][:, jj * 128:(jj + 1) * 128], in_=kfs[kb_][:, b * 128:(b + 1) * 128],
                                                             identity=self.identf), reads=[bkfs[kb_], self.bC], writes=[self.bPS[pt]])
                tk.op(tk.act, lambda: nc.scalar.activation(out=outk[kb_][:, q4 * 512:(q4 + 1) * 512], in_=self.PS[pt][:, :], func=AF.Copy),
                      reads=[self.bPS[pt]], writes=[boutk[kb_]], self_ok=True)
                if kind == "v":
                    tk.op(tk.pool, lambda: nc.gpsimd.tensor_copy(out=kts[kb_][:, q4 * 512:(q4 + 1) * 512], in_=outk[kb_][:, q4 * 512:(q4 + 1) * 512]),
                          reads=[boutk[kb_]], writes=[bkts[kb_]])
            dst = self.k_out if kind == "k" else self.v_out
            tk.dma(tk.pool, dst.rearrange("(s b p) c -> s p b c", p=128, b=NB)[slot, :, :, h * 128:(h + 1) * 128],
                   outk[kb_].rearrange("p (b c) -> p b c", c=128), reads=[boutk[kb_]], writes=[self.b_out], dbuf=boutk[kb_])
            if kind == "v":
                r0 = (slot * H + h) * 128
                tk.dma(tk.pool, self.v_hist[r0:r0 + 128, :], kts[kb_], reads=[bkts[kb_]], writes=[self.b_vh[slot][h]], dbuf=bkts[kb_])

        for i in range(len(items)):
            stage_a(i)
            if i >= 1:
                stage_b(i - 1)
        stage_b(len(items) - 1)

        lf = tmp[0:8, 0:1024]
        ct = tmp[0:8, 1024:2048]
        lfo = tmp[:, 2048:2112]
        cto = tmp[:, 2112:2176]
        rhs8 = tmp[0:8, 2176:2240]
        blf, bct, blfo, bcto, brhs8 = Buf("lf"), Buf("ct"), Buf("lfo"), Buf("cto"), Buf("rhs8")
        bones = Buf("ones8")
        self.tmp_take([blf, bct, blfo, bcto, brhs8, bones])
        self.bones = bones
        import os
        ksub = int(os.environ.get("KSUB", "99"))
        if ksub == 1:
            return
        for half in range(2):
            pb = cnt % 2
            cnt += 1
            proj(self.WF[:], self.bC, half * 512, 512, pb, m=8, wcols=8)
            hs = slice(half * 512, (half + 1) * 512)
            tk.op(tk.act, lambda: nc.scalar.activation(out=lf[:, hs], in_=self.PS[pb][0:8, :], func=AF.Exp, scale=-1.0, bias=self.nbfc),
                  reads=[self.bPS[pb], self.bC], writes=[blf])
            tk.op(tk.act, lambda: nc.scalar.activation(out=lf[:, hs], in_=lf[:, hs], func=AF.Ln, scale=1.0, bias=1.0),
                  reads=[blf], writes=[blf])
            tk.op(tk.dve, lambda: nc.vector.tensor_scalar(out=lf[:, hs], in0=lf[:, hs], scalar1=-1.0, scalar2=None, op0=ALU.mult),
                  reads=[blf], writes=[blf])
        if ksub == 2:
            return
        self.cumsum_logf(slot, lf, ct, lfo, cto, rhs8, blf, bct, blfo, bcto, brhs8, T, want_hs=full)
        if ksub == 3:
            return
        if full:
            self.pool_chunks(slot, ids, last_full)
        else:
            for c in range(8):
                lid, j = ids[("p", c)]
                sl, bsl = self.acquire(lid)
                pb = cnt % 2
                cnt += 1
                proj(sl[:, j * 2048:(j + 1) * 2048], bsl, T - 128, 128, pb)
                tk.op(tk.act, lambda: nc.scalar.activation(out=self.HALO[:, c * 16 + 1:c * 16 + 16], in_=self.PS[pb][:, 113:128], func=AF.Copy),
                      reads=[self.bPS[pb]], writes=[self.bHALO])
        if full:
            self.qt_take(self.bQT)
            for h in range(H):
                lid, j = ids[("q", h)]
                sl, bsl = self.acquire(lid)
                for half in range(2):
                    pb = cnt % 2
                    cnt += 1
                    proj(sl[:, j * 2048:(j + 1) * 2048], bsl, half * 512, 512, pb)
                    tk.op(tk.act, lambda: nc.scalar.activation(out=self.QT[:, h * T + half * 512: h * T + (half + 1) * 512], in_=self.PS[pb][:, :],
                                                               func=AF.Copy), reads=[self.bPS[pb]], writes=[self.bQT[h]])

    def cumsum_logf(self, slot, lf, ct, lfo, cto, rhs8, blf, bct, blfo, bcto, brhs8, ntok, want_hs=False):
        nc, tk = self.nc, self.tk
        nblk = ntok // 128
        ones = self.TMP[0:8, 2240:2240 + 512]
        bones = self.bones
        tk.op(tk.dve, lambda: nc.vector.memset(ones, 1.0), writes=[bones])
        for c0 in range(0, ntok, 512):
            n = min(512, ntok - c0)
            tk.op(tk.dve, lambda: nc.vector.tensor_tensor_scan(out=ct[:, c0:c0 + n], data0=ones[:, 0:n], data1=lf[:, c0:c0 + n],
                                                               initial=self.cprev, op0=ALU.mult, op1=ALU.add),
                  reads=[blf, bones, self.bC], writes=[bct])
            tk.op(tk.dve, lambda: nc.vector.tensor_copy(out=self.cprev, in_=ct[:, c0 + n - 1:c0 + n]), reads=[bct], writes=[self.bC])
        for b in range(nblk):
            tk.op(tk.pe, lambda: nc.tensor.transpose(out=self.PS[2][:, b * 8:(b + 1) * 8], in_=lf[:, b * 128:(b + 1) * 128],
                                                     identity=self.identf[0:8, 0:8]), reads=[blf, self.bC], writes=[self.bPS[2]])
            tk.op(tk.pe, lambda: nc.tensor.transpose(out=self.PS[3][:, b * 8:(b + 1) * 8], in_=ct[:, b * 128:(b + 1) * 128],
                                                     identity=self.identf[0:8, 0:8]), reads=[bct, self.bC], writes=[self.bPS[3]])
        tk.op(tk.act, lambda: nc.scalar.activation(out=lfo[:, 0:nblk * 8], in_=self.PS[2][:, 0:nblk * 8], func=AF.Copy),
              reads=[self.bPS[2]], writes=[blfo])
        if want_hs:
            h1t = ones.bitcast(BF16)
            tk.op(tk.dve, lambda: nc.vector.tensor_scalar(out=lf, in0=ct, scalar1=1.0 / SCALE, scalar2=None, op0=ALU.mult),
                  reads=[bct], writes=[blf])
            tk.op(tk.dve, lambda: nc.vector.tensor_copy(out=self.HS[0:8, :], in_=lf), reads=[blf], writes=[self.bHS])
            tk.op(tk.dve, lambda: nc.vector.tensor_tensor(out=lf, in0=lf, in1=self.HS[0:8, :], op=ALU.subtract), reads=[blf, self.bHS], writes=[blf])
            tk.op(tk.dve, lambda: nc.vector.tensor_copy(out=h1t, in_=lf), reads=[blf], writes=[bones])
            tk.dma(tk.sp, self.HS[8:16, :], h1t, reads=[bones], writes=[self.bHS], dbuf=bones)
        if slot is not None:
            tk.dma(tk.pool, self.lf_out[slot * 128:(slot + 1) * 128, :], lfo, reads=[blfo], writes=[self.b_out], dbuf=blfo)
            tk.op(tk.dve, lambda: nc.vector.tensor_copy(
                out=self.CK[:].rearrange("p (h k) -> p k h", k=64)[:, slot * 8:(slot + 1) * 8, :],
                in_=self.PS[3][:, 0:64].rearrange("p (b h) -> p b h", h=8)), reads=[self.bPS[3]], writes=[self.bCK])
            for h in range(H):
                tk.op(tk.dve, lambda: nc.vector.tensor_tensor(out=rhs8[:, h * 8:(h + 1) * 8], in0=self.diag8[:, h * 8:(h + 1) * 8],
                                                              in1=ct[:, 127:ntok:128], op=ALU.mult), reads=[bct, self.bC], writes=[brhs8])
            tk.op(tk.pe, lambda: nc.tensor.matmul(self.PS[2][:, 64:128], lhsT=self.onesf[0:8, :], rhs=rhs8, start=True, stop=True),
                  reads=[brhs8, self.bC], writes=[self.bPS[2]])
            tk.op(tk.dve, lambda: nc.vector.tensor_copy(out=self.CR[:], in_=self.PS[2][:, 64:128]), reads=[self.bPS[2]], writes=[self.bCR])

    def pool_chunks(self, slot, ids, last_full):
        nc, tk = self.nc, self.tk
        qt = self.QT
        W = 16 + 512
        p0 = qt[:, 0:2 * W].bitcast(F32)
        tb = qt[:, 2 * W:4 * W].bitcast(F32)
        tc_ = qt[:, 4 * W:6 * W].bitcast(F32)
        bp0, btb, btc = Buf("p0"), Buf("tb"), Buf("tc")
        self.qt_take([bp0, btb, btc])
        db = self.HM
        bdb = [Buf("db")]
        first = True
        cnt = 0
        for c in range(8):
            g = c // 2
            w = 2 << g
            lid, j = ids[("p", c)]
            sl, bsl = self.acquire(lid)
            for half in range(2):
                pb = cnt % 2
                cnt += 1
                for kc in range(KC):
                    tk.op(tk.pe, lambda: nc.tensor.matmul(self.PS[pb][:, :], lhsT=sl[:, j * 2048 + kc * 128: j * 2048 + (kc + 1) * 128],
                                                          rhs=self.XT[:, kc * T + half * 512: kc * T + (half + 1) * 512],
                                                          start=(kc == 0), stop=(kc == KC - 1)),
                          reads=[bsl, self.bXT], writes=[self.bPS[pb]], signal=(kc == KC - 1))
                if half == 0:
                    tk.op(tk.dve, lambda: nc.vector.tensor_copy(out=p0[:, 0:16], in_=self.HALO[:, c * 16:(c + 1) * 16]),
                          reads=[self.bHALO], writes=[bp0])
                else:
                    tk.op(tk.dve, lambda: nc.vector.tensor_copy(out=p0[:, 0:16], in_=p0[:, 512:528]), reads=[bp0], writes=[bp0])
                tk.op(tk.act, lambda: nc.scalar.activation(out=p0[:, 16:W], in_=self.PS[pb][:, :], func=AF.Copy),
                      reads=[self.bPS[pb]], writes=[bp0])
                src, bsrc = p0, bp0
                step = 1
                k = 0
                while step < w:
                    dst, bdst = (tb, btb) if k % 2 == 0 else (tc_, btc)
                    lo = 2 * step
                    tk.op(tk.dve, lambda: nc.vector.tensor_tensor(out=dst[:, lo:W], in0=src[:, lo:W], in1=src[:, lo - step:W - step], op=ALU.add),
                          reads=[bsrc], writes=[bdst])
                    src, bsrc = dst, bdst
                    step *= 2
                    k += 1
                if slot == 1 and half == 0:
                    tk.op(tk.dve, lambda: nc.vector.tensor_tensor(out=src[:, 16:32], in0=src[:, 16:32], in1=self.poolfix[:, g * 16:(g + 1) * 16],
                                                                  op=ALU.mult), reads=[bsrc, self.bC], writes=[bsrc])
                if first:
                    self.hm_take(bdb)
                    first = False
                tk.op(tk.dve, lambda: nc.vector.scalar_tensor_tensor(out=db[:, c * T + half * 512: c * T + (half + 1) * 512], in0=src[:, 16:W],
                                                                     scalar=1.0 / w, in1=p0[:, 16:W], op0=ALU.mult, op1=ALU.subtract),
                      reads=[bsrc, bp0], writes=bdb)
                if last_full and half == 1:
                    tk.op(tk.pe, lambda: nc.tensor.transpose(out=self.PS[2][0:15, c * 128:(c + 1) * 128] if c < 4 else self.PS[3][0:15, (c - 4) * 128:(c - 3) * 128],
                                                             in_=p0[:, W - 15:W], identity=self.identf),
                          reads=[bp0, self.bC], writes=[self.bPS[2] if c < 4 else self.bPS[3]])
        if last_full:
            pso = self.TMP[0:15, 2048:3072]
            bpso = Buf("pso")
            self.tmp_take([bpso])
            tk.op(tk.act, lambda: nc.scalar.activation(out=pso[:, 0:512], in_=self.PS[2][0:15, :], func=AF.Copy), reads=[self.bPS[2]], writes=[bpso])
            tk.op(tk.act, lambda: nc.scalar.activation(out=pso[:, 512:1024], in_=self.PS[3][0:15, :], func=AF.Copy), reads=[self.bPS[3]], writes=[bpso])
            tk.dma(tk.pool, self.pool_out, pso, reads=[bpso], writes=[self.b_out], dbuf=bpso)
        self.bDB = bdb[0]

    def attn_loads(self, s):
        tk = self.tk
        ids = {}
        for h in range(H):
            for ks in range(s + 1):
                def ld(slot, buf, h=h, ks=ks):
                    r0 = (ks * H + h) * 128
                    tk.dma(tk.sp, slot[:, 0:1024], self.kt_hist[r0:r0 + 128, :], reads=[self.b_kth[ks][h]], writes=[buf])
                    tk.dma(tk.sp, slot[:, 1024:2048], self.v_hist[r0:r0 + 128, :], reads=[self.b_vh[ks][h]], writes=[buf])
                ids[(h, ks)] = self.add_load(ld)
        return ids

    def attention(self, s, ids):
        nc, tk = self.nc, self.tk
        rl = self.TMP[:, 0:512]
        brl = Buf("rl")
        self.tmp_take([brl])
        iters = []
        for h in range(H):
            for ks in range(s + 1):
                for kb in range(NB):
                    for half in range(2):
                        qb0 = half * 4
                        if ks == s:
                            if kb > qb0 + 3:
                                continue
                            first_q = max(kb, qb0)
                        else:
                            first_q = qb0
                        last = (ks == s) and (kb == min(NB - 1, qb0 + 3))
                        iters.append(dict(h=h, ks=ks, kb=kb, half=half, qb0=qb0, first_q=first_q, col0=(first_q - qb0) * 128, last=last,
                                          headend=(ks == s and kb == NB - 1 and half == 1)))
        state = {"sl": {}, "head": None}
        started = {}

        def get_slot(h, ks):
            key = (h, ks)
            if key not in state["sl"]:
                state["sl"] = {key: self.acquire(ids[key])}
            return state["sl"][key]

        def emit_scores(i, it):
            h, ks, kb, half, col0 = it["h"], it["ks"], it["kb"], it["half"], it["col0"]
            sl, bsl = get_slot(h, ks)
            it["sl"], it["bsl"] = sl, bsl
            bk = h % 2
            if state["head"] != h:
                tk.op(tk.dve, lambda: nc.vector.tensor_tensor(out=self.BTK[:, bk * 64:(bk + 1) * 64], in0=self.kmask[:, 0:64],
                                                              in1=self.CK[:, h * 64:(h + 1) * 64], op=ALU.subtract),
                      reads=[self.bCK, self.bC], writes=[self.bBTK[bk]], self_ok=True)
                state["head"] = h
            ps = i % 2
            tk.op(tk.pe, lambda: nc.tensor.matmul(self.PS[ps][:, col0:512], lhsT=sl[:, kb * 128:(kb + 1) * 128],
                                                  rhs=self.QT[:, h * T + half * 512 + col0: h * T + (half + 1) * 512],
                                                  start=True, stop=False), reads=[bsl, self.bQT[h]], writes=[self.bPS[ps]], signal=False)
            tk.op(tk.pe, lambda: nc.tensor.matmul(self.PS[ps][:, col0:512], lhsT=self.SELB[:, h * 128:(h + 1) * 128],
                                                  rhs=self.HS[:, half * 512 + col0:(half + 1) * 512],
                                                  start=False, stop=True), reads=[self.bC, self.bHS], writes=[self.bPS[ps]], signal=True)

        def emit_rest(i, it):
            h, ks, kb, half, col0, qb0, first_q = it["h"], it["ks"], it["kb"], it["half"], it["col0"], it["qb0"], it["first_q"]
            sl, bsl = it["sl"], it["bsl"]
            bk = h % 2
            gkb = ks * NB + kb
            ps = i % 2
            pt = i % 3
            ptile = self.PT[:, pt * 512:(pt + 1) * 512]
            tk.op(tk.act, lambda: nc.scalar.activation(out=ptile[:, col0:512], in_=self.PS[ps][:, col0:512], func=AF.Exp, scale=SCALE,
                                                       bias=self.BTK[:, bk * 64 + gkb:bk * 64 + gkb + 1]),
                  reads=[self.bPS[ps], self.bBTK[bk]], writes=[self.bPT[pt]], self_ok=True)
            if ks == s and first_q == kb:
                tk.op(tk.pool, lambda: nc.gpsimd.tensor_tensor(out=ptile[:, col0:col0 + 128], in0=ptile[:, col0:col0 + 128],
                                                               in1=self.trib, op=ALU.mult), reads=[self.bPT[pt], self.bC], writes=[self.bPT[pt]])
            st_ = not started.get((h, half), False)
            tk.op(tk.pe, lambda: nc.tensor.matmul(self.PS[2 + half][:, col0:512], lhsT=self.ONESB[:], rhs=ptile[:, col0:512],
                                                  start=st_, stop=it["last"]),
                  reads=[self.bC, self.bPT[pt]], writes=[self.bPS[2 + half]], signal=False)
            tk.op(tk.pe, lambda: nc.tensor.matmul(self.PS[4 + half][:, col0:512], lhsT=sl[:, 1024 + kb * 128:1024 + (kb + 1) * 128],
                                                  rhs=ptile[:, col0:512], start=st_, stop=it["last"]),
                  reads=[bsl, self.bPT[pt]], writes=[self.bPS[4 + half]], signal=True)
            started[(h, half)] = True
            if it["headend"]:
                for hf in range(2):
                    tk.op(tk.dve, lambda: nc.vector.reciprocal(out=rl, in_=self.PS[2 + hf][:, :]), reads=[self.bPS[2 + hf]], writes=[brl])
                    tk.op(tk.dve, lambda: nc.vector.tensor_tensor(out=self.XT[:, h * T + hf * 512: h * T + (hf + 1) * 512],
                                                                  in0=self.PS[4 + hf][:, :], in1=rl, op=ALU.mult),
                          reads=[self.bPS[4 + hf], brl], writes=[self.bXT])

        emit_scores(0, iters[0])
        for i, it in enumerate(iters):
            if i + 1 < len(iters):
                emit_scores(i + 1, iters[i + 1])
            emit_rest(i, it)

    def mix_loads(self):
        tk = self.tk
        ids = {}

        def ldp(slot, buf):
            tk.dma(tk.sp, slot[:, 0:2048].rearrange("p (a c) -> p a c", c=256),
                   self.wp_s.rearrange("(a p) c -> p a c", p=128), reads=[self.b_wp], writes=[buf])
        ids["wp"] = self.add_load(ldp)
        for slab in range(8):
            def ld(slot, buf, slab=slab):
                tk.dma(tk.sp, slot[:, 0:4096].rearrange("p (k c) -> p k c", c=256),
                       self.wo_s[:, slab * 256:(slab + 1) * 256].rearrange("(k p) c -> p k c", p=128), reads=[self.b_wo], writes=[buf])
            ids[("wo", slab)] = self.add_load(ld)
        return ids

    def mix(self, ids, ncol=T, db=None):
        nc, tk = self.nc, self.tk
        if db is None:
            db = self.HM[:]
        sl, bsl = self.acquire(ids["wp"])
        cnt = 0
        halves = [(c0, min(512, ncol - c0)) for c0 in range(0, ncol, 512)]
        nblk = (ncol + 127) // 128
        for g in range(4):
            for oc in range(2):
                for (c0, n) in halves:
                    pb = cnt % 2
                    cnt += 1
                    for k2 in range(2):
                        a = g * 2 + k2
                        tk.op(tk.pe, lambda: nc.tensor.matmul(self.PS[pb][:, 0:n], lhsT=sl[:, a * 256 + oc * 128: a * 256 + (oc + 1) * 128],
                                                              rhs=db[:, (g * 2 + k2) * ncol + c0:(g * 2 + k2) * ncol + c0 + n],
                                                              start=(k2 == 0), stop=(k2 == 1)),
                              reads=[bsl, self.bDB], writes=[self.bPS[pb]], signal=(k2 == 1))
                    ch = 8 + g * 2 + oc
                    tk.op(tk.act, lambda: nc.scalar.activation(out=self.XT[:, ch * ncol + c0: ch * ncol + c0 + n], in_=self.PS[pb][:, 0:n],
                                                               func=AF.Copy, scale=self.psc[:, g * 2 + oc: g * 2 + oc + 1]),
                          reads=[self.bPS[pb], self.bC], writes=[self.bXT])
        for slab in range(8):
            sl, bsl = self.acquire(ids[("wo", slab)])
            for b in range(nblk):
                nt = min(128, ncol - b * 128)
                pb = 2 + cnt % 2
                cnt += 1
                for kc in range(KC):
                    tk.op(tk.pe, lambda: nc.tensor.matmul(self.PS[pb][0:nt, 0:256], lhsT=self.XT[:, kc * ncol + b * 128: kc * ncol + b * 128 + nt],
                                                          rhs=sl[:, kc * 256:(kc + 1) * 256], start=(kc == 0), stop=(kc == KC - 1)),
                          reads=[bsl, self.bXT], writes=[self.bPS[pb]], signal=(kc == KC - 1))
                Rs = self.R[0:nt, b * D + slab * 256: b * D + (slab + 1) * 256]
                tk.op(tk.dve, lambda: nc.vector.tensor_tensor(out=Rs, in0=Rs, in1=self.PS[pb][0:nt, 0:256], op=ALU.add),
                      reads=[self.bPS[pb], self.bR[b][slab // 2]], writes=[self.bR[b][slab // 2]])

    def final_out(self, dst_rows, nblk=NB, npart=128):
        nc, tk = self.nc, self.tk
        sq = self.TMP[:, 0:1024].bitcast(BF16)
        bsq = Buf("sqf")
        self.tmp_take([bsq])
        st = self.ST[0:npart, :]
        sq = sq[0:npart, :]
        for b in range(nblk):
            Rb = self.R[0:npart, b * D:(b + 1) * D]
            tk.op(tk.act, lambda: nc.scalar.activation(out=sq, in_=Rb, func=AF.Square, accum_out=st[:, 3 * b:3 * b + 1]),
                  reads=self.bR[b], writes=[bsq, self.bST])
            tk.op(tk.act, lambda: nc.scalar.activation(out=st[:, 3 * b + 1:3 * b + 2], in_=st[:, 3 * b:3 * b + 1], func=AF.Sqrt,
                                                       scale=1.0 / D, bias=EPS), reads=[self.bST], writes=[self.bST])
            tk.op(tk.dve, lambda: nc.vector.reciprocal(out=st[:, 3 * b + 2:3 * b + 3], in_=st[:, 3 * b + 1:3 * b + 2]),
                  reads=[self.bST], writes=[self.bST])
            tk.op(tk.dve, lambda: nc.vector.scalar_tensor_tensor(out=Rb, in0=Rb, scalar=st[:, 3 * b + 2:3 * b + 3], in1=self.GFIN[0:npart, :],
                                                                 op0=ALU.mult, op1=ALU.mult), reads=self.bR[b] + [self.bST, self.bC], writes=self.bR[b])
            tk.dma(tk.pool, dst_rows(b), Rb, reads=self.bR[b], writes=[self.b_out], dbuf=self.bRst[b])

    def sample_attn_loads(self):
        tk = self.tk
        ids = {}
        for h in range(H):
            for sq_ in range(2):
                def ld(slot, buf, h=h, sq_=sq_):
                    r0 = (sq_ * H + h) * 128
                    tk.dma(tk.sp, slot[:, 0:2048].bitcast(F32), self.ckT[r0:r0 + 128, :], writes=[buf])
                    tk.dma(tk.sp, slot[:, 2048:4096].bitcast(F32), self.cv[r0:r0 + 128, :], writes=[buf])
                ids[(h, sq_)] = self.add_load(ld)
        return ids

    def sample_slot(self, plan):
        nc, tk = self.nc, self.tk
        NS = 64
        tk.dma(tk.sp, self.R[0:64, 0:D], self.xsmp, writes=self.bR[0], dbuf=self.bRld[0])
        self.norm_to_xt(0, nblk=1, ntok=NS, npart=NS)
        self.ffn(plan["ffn1"], ncol=NS)
        self.norm_to_xt(1, nblk=1, ntok=NS, npart=NS)
        ids = plan["inproj"]
        hm = self.HM
        kf = hm[:, 0:128].bitcast(F32)
        kn = hm[:, 128:640]
        vn = hm[:, 640:1664]
        kout = hm[:, 1664:3712].bitcast(F32)
        vout = hm[:, 3712:5760].bitcast(F32)
        dbs = hm[:, 5760:6272]
        bkf, bkn, bvn, bkout, bvout, bdbs = Buf("skf"), Buf("skn"), Buf("svn"), Buf("skout"), Buf("svout"), Buf("sdbs")
        self.hm_take([bkf, bkn, bvn, bkout, bvout, bdbs])
        if not getattr(self, "wf_loaded", False):
            self.load_wf()
            self.wf_loaded = True
        tmp = self.TMP
        cc = [tmp[0:8, 0:1024], tmp[0:8, 1024:2048]]
        ones = tmp[0:8, 2048:3072]
        bcc, bones = Buf("scc"), Buf("sones")
        self.tmp_take([bcc, bones])
        smp = self.SMP
        lfn = smp[0:8, 0:64]
        cn = smp[0:8, 64:128]
        rhs8 = smp[0:8, 128:144]
        cks = smp[:, 144:272]
        crs = smp[:, 272:288]
        cnT = smp[0:64, 288:296]
        crn = smp[0:64, 296:304]
        btc = smp[:, 304:432]
        btn = smp[0:64, 432:440]
        lfs = smp[0:64, 440:448]
        bsm = Buf("smp")
        bspt, bm64 = Buf("spt"), Buf("m64")
        tk.dma(tk.sp, self.SPT[:], self.spT, writes=[bspt])
        tk.dma(tk.sp, self.M64[0:64, :], self.mask64_d, writes=[bm64])
        tk.dma(tk.sp, cc[0], self.clfT[0:8, :], writes=[bcc])
        tk.dma(tk.sp, cc[1], self.clfT[8:16, :], writes=[bcc])
        tk.op(tk.dve, lambda: nc.vector.memset(ones, 1.0), writes=[bones])
        cnt = 0

        def proj(w_ap, bw, pbank, m=128, wcols=128):
            for kc in range(KC):
                tk.op(tk.pe, lambda: nc.tensor.matmul(self.PS[pbank][0:m, 0:NS], lhsT=w_ap[:, kc * wcols: kc * wcols + m],
                                                      rhs=self.XT[:, kc * NS:(kc + 1) * NS], start=(kc == 0), stop=(kc == KC - 1)),
                      reads=[bw, self.bXT], writes=[self.bPS[pbank]], signal=(kc == KC - 1))
        for kind in ("k", "v"):
            for h in range(H):
                lid, j = ids[(kind, h)]
                sl, bsl = self.acquire(lid)
                pb = cnt % 2
                cnt += 1
                proj(sl[:, j * 2048:(j + 1) * 2048], bsl, pb)
                tk.op(tk.dve, lambda: nc.vector.tensor_copy(out=kf, in_=self.PS[pb][:, 0:NS]), reads=[self.bPS[pb]], writes=[bkf])
                if kind == "k":
                    tk.op(tk.act, lambda: nc.scalar.activation(out=kn[:, h * NS:(h + 1) * NS], in_=kf, func=AF.Copy), reads=[bkf], writes=[bkn])
                pt = 2 + h // 4
                tk.op(tk.pe, lambda: nc.tensor.transpose(out=self.PS[pt][0:NS, (h % 4) * 128:(h % 4 + 1) * 128], in_=kf, identity=self.identf),
                      reads=[bkf, self.bC], writes=[self.bPS[pt]])
            o, bo = (kout, bkout) if kind == "k" else (vout, bvout)
            for q4 in range(2):
                tk.op(tk.act, lambda: nc.scalar.activation(out=o[0:NS, q4 * 512:(q4 + 1) * 512], in_=self.PS[2 + q4][0:NS, :], func=AF.Copy),
                      reads=[self.bPS[2 + q4]], writes=[bo])
            tk.dma(tk.pool, self.ks_out if kind == "k" else self.vs_out, o[0:NS, :], reads=[bo], writes=[self.b_out], dbuf=bo)
            if kind == "v":
                tk.op(tk.pool, lambda: nc.gpsimd.tensor_copy(out=vn[0:NS, :], in_=vout[0:NS, :]), reads=[bvout], writes=[bvn])
        pb = cnt % 2
        cnt += 1
        proj(self.WF[:], self.bC, pb, m=8, wcols=8)
        tk.op(tk.act, lambda: nc.scalar.activation(out=lfn, in_=self.PS[pb][0:8, 0:NS], func=AF.Exp, scale=-1.0, bias=self.nbfc),
              reads=[self.bPS[pb], self.bC], writes=[bsm])
        tk.op(tk.act, lambda: nc.scalar.activation(out=lfn, in_=lfn, func=AF.Ln, scale=1.0, bias=1.0), reads=[bsm], writes=[bsm])
        tk.op(tk.dve, lambda: nc.vector.tensor_scalar(out=lfn, in0=lfn, scalar1=-1.0, scalar2=None, op0=ALU.mult), reads=[bsm], writes=[bsm])
        for sq_ in range(2):
            for c0 in range(0, 1024, 512):
                init = 0.0 if c0 == 0 else cc[sq_][:, c0 - 1:c0]
                tk.op(tk.dve, lambda: nc.vector.tensor_tensor_scan(out=cc[sq_][:, c0:c0 + 512], data0=ones[:, 0:512], data1=cc[sq_][:, c0:c0 + 512],
                                                                   initial=init, op0=ALU.mult, op1=ALU.add), reads=[bcc, bones], writes=[bcc])
            tk.op(tk.dve, lambda: nc.vector.tensor_tensor_scan(out=cn[:, sq_ * 32:(sq_ + 1) * 32], data0=ones[:, 0:32], data1=lfn[:, sq_ * 32:(sq_ + 1) * 32],
                                                               initial=cc[sq_][:, 1023:1024], op0=ALU.mult, op1=ALU.add),
                  reads=[bcc, bones, bsm], writes=[bsm])
            for kb in range(NB):
                tk.op(tk.pe, lambda: nc.tensor.transpose(out=self.PS[3][:, (sq_ * 8 + kb) * 8:(sq_ * 8 + kb + 1) * 8], in_=cc[sq_][:, kb * 128:(kb + 1) * 128],
                                                         identity=self.identf[0:8, 0:8]), reads=[bcc, self.bC], writes=[self.bPS[3]])
            tk.op(tk.dve, lambda: nc.vector.tensor_scalar(out=rhs8[:, sq_ * 8:(sq_ + 1) * 8], in0=self.identf[0:8, 0:8],
                                                          scalar1=cn[:, sq_ * 32 + 31:sq_ * 32 + 32], scalar2=None, op0=ALU.mult),
                  reads=[bsm, self.bC], writes=[bsm])
        tk.op(tk.dve, lambda: nc.vector.tensor_copy(out=cks, in_=self.PS[3][:, 0:128]), reads=[self.bPS[3]], writes=[bsm])
        tk.op(tk.pe, lambda: nc.tensor.matmul(self.PS[2][:, 0:16], lhsT=self.onesf[0:8, :], rhs=rhs8, start=True, stop=True),
              reads=[bsm, self.bC], writes=[self.bPS[2]])
        tk.op(tk.dve, lambda: nc.vector.tensor_copy(out=crs, in_=self.PS[2][:, 0:16]), reads=[self.bPS[2]], writes=[bsm])
        tk.op(tk.pe, lambda: nc.tensor.transpose(out=self.PS[3][0:NS, 128:136], in_=cn, identity=self.identf[0:8, 0:8]),
              reads=[bsm, self.bC], writes=[self.bPS[3]])
        tk.op(tk.pe, lambda: nc.tensor.transpose(out=self.PS[3][0:NS, 136:144], in_=lfn, identity=self.identf[0:8, 0:8]),
              reads=[bsm, self.bC], writes=[self.bPS[3]])
        tk.op(tk.dve, lambda: nc.vector.tensor_copy(out=cnT, in_=self.PS[3][0:NS, 128:136]), reads=[self.bPS[3]], writes=[bsm])
        tk.op(tk.dve, lambda: nc.vector.tensor_copy(out=lfs, in_=self.PS[3][0:NS, 136:144]), reads=[self.bPS[3]], writes=[bsm])
        tk.dma(tk.pool, self.lfs_out, lfs, reads=[bsm], writes=[self.b_out], dbuf=bsm)
        tk.op(tk.dve, lambda: nc.vector.tensor_copy(out=crn[0:32, :], in_=crs[0:32, 0:8]), reads=[bsm], writes=[bsm])
        tk.op(tk.dve, lambda: nc.vector.tensor_copy(out=crn[32:64, :], in_=crs[32:64, 8:16]), reads=[bsm], writes=[bsm])
        tk.op(tk.dve, lambda: nc.vector.tensor_tensor(out=btn, in0=crn, in1=cnT, op=ALU.subtract), reads=[bsm], writes=[bsm])
        for sq_ in range(2):
            for kb in range(NB):
                o_ = (sq_ * 8 + kb) * 8
                tk.op(tk.dve, lambda: nc.vector.tensor_tensor(out=btc[:, o_:o_ + 8], in0=crs[:, sq_ * 8:(sq_ + 1) * 8], in1=cks[:, o_:o_ + 8],
                                                              op=ALU.subtract), reads=[bsm], writes=[bsm])
        qt = self.QT
        W = 96
        p0 = qt[:, 0:2 * W].bitcast(F32)
        tb = qt[:, 2 * W:4 * W].bitcast(F32)
        tc_ = qt[:, 4 * W:6 * W].bitcast(F32)
        bp0, btb, btcb = Buf("sp0"), Buf("stb"), Buf("stc")
        self.qt_take([bp0, btb, btcb])
        pso = self.TMP[0:15, 0:2048]
        bpso = Buf("spso")
        first_pso = True
        for c in range(8):
            g = c // 2
            w = 2 << g
            lid, j = ids[("p", c)]
            sl, bsl = self.acquire(lid)
            pb = cnt % 2
            cnt += 1
            proj(sl[:, j * 2048:(j + 1) * 2048], bsl, pb)
            for sq_ in range(2):
                tk.op(tk.dve, lambda: nc.vector.tensor_copy(out=p0[:, sq_ * 48:sq_ * 48 + 16], in_=self.SPT[:, (sq_ * 8 + c) * 16:(sq_ * 8 + c + 1) * 16]),
                      reads=[bspt], writes=[bp0])
                tk.op(tk.act, lambda: nc.scalar.activation(out=p0[:, sq_ * 48 + 16:sq_ * 48 + 48], in_=self.PS[pb][:, sq_ * 32:(sq_ + 1) * 32], func=AF.Copy),
                      reads=[self.bPS[pb]], writes=[bp0])
            src, bsrc = p0, bp0
            step = 1
            k = 0
            while step < w:
                dst, bdst = (tb, btb) if k % 2 == 0 else (tc_, btcb)
                lo = 2 * step
                tk.op(tk.dve, lambda: nc.vector.tensor_tensor(out=dst[:, lo:W], in0=src[:, lo:W], in1=src[:, lo - step:W - step], op=ALU.add),
                      reads=[bsrc], writes=[bdst])
                src, bsrc = dst, bdst
                step *= 2
                k += 1
            for sq_ in range(2):
                tk.op(tk.dve, lambda: nc.vector.scalar_tensor_tensor(out=dbs[:, c * NS + sq_ * 32:c * NS + (sq_ + 1) * 32], in0=src[:, sq_ * 48 + 16:sq_ * 48 + 48],
                                                                     scalar=1.0 / w, in1=p0[:, sq_ * 48 + 16:sq_ * 48 + 48], op0=ALU.mult, op1=ALU.subtract),
                      reads=[bsrc, bp0], writes=[bdbs])
                pbk = 2 + sq_ * 2 + c // 4
                tk.op(tk.pe, lambda: nc.tensor.transpose(out=self.PS[pbk][0:15, (c % 4) * 128:(c % 4 + 1) * 128], in_=p0[:, sq_ * 48 + 33:sq_ * 48 + 48],
                                                         identity=self.identf), reads=[bp0, self.bC], writes=[self.bPS[pbk]])
        self.tmp_take([bpso])
        for sq_ in range(2):
            for hf in range(2):
                tk.op(tk.act, lambda: nc.scalar.activation(out=pso[:, hf * 512:(hf + 1) * 512] if sq_ == 0 else pso[:, 1024 + hf * 512:1024 + (hf + 1) * 512],
                                                           in_=self.PS[2 + sq_ * 2 + hf][0:15, :], func=AF.Copy), reads=[self.bPS[2 + sq_ * 2 + hf]], writes=[bpso])
            tk.dma(tk.pool, self.ps_out[sq_ * 15:(sq_ + 1) * 15, :], pso[:, sq_ * 1024:(sq_ + 1) * 1024], reads=[bpso], writes=[self.b_out], dbuf=bpso)
        self.qt_take(self.bQT)
        for h in range(H):
            lid, j = ids[("q", h)]
            sl, bsl = self.acquire(lid)
            pb = cnt % 2
            cnt += 1
            proj(sl[:, j * 2048:(j + 1) * 2048], bsl, pb)
            tk.op(tk.act, lambda: nc.scalar.activation(out=self.QT[:, h * T:h * T + NS], in_=self.PS[pb][:, 0:NS], func=AF.Copy),
                  reads=[self.bPS[pb]], writes=[self.bQT[h]])
        aids = plan["attn"]
        kvb = [self.TMP[:, 1024 + i * 1024:1024 + (i + 1) * 1024].bitcast(BF16) for i in range(2)]
        bkvb = [Buf("skvb0"), Buf("skvb1")]
        rl = self.TMP[:, 0:64]
        brl = Buf("srl")
        self.tmp_take(bkvb + [brl])
        acnt = 0
        for h in range(H):
            started = False
            for sq_ in range(2):
                sl, bsl = self.acquire(aids[(h, sq_)])
                kv = kvb[acnt % 2]
                bkv = bkvb[acnt % 2]
                acnt += 1
                tk.op(tk.act, lambda: nc.scalar.activation(out=kv[:, 0:1024], in_=sl[:, 0:2048].bitcast(F32), func=AF.Copy), reads=[bsl], writes=[bkv])
                tk.op(tk.pool, lambda: nc.gpsimd.tensor_copy(out=kv[:, 1024:2048], in_=sl[:, 2048:4096].bitcast(F32)), reads=[bsl], writes=[bkv])
                qs = slice(sq_ * 32, (sq_ + 1) * 32)
                for kb in range(NB):
                    ps = cnt % 2
                    pt = cnt % 3
                    cnt += 1
                    tk.op(tk.pe, lambda: nc.tensor.matmul(self.PS[ps][:, 0:32], lhsT=kv[:, kb * 128:(kb + 1) * 128],
                                                          rhs=self.QT[:, h * T + sq_ * 32:h * T + (sq_ + 1) * 32], start=True, stop=True),
                          reads=[bkv, self.bQT[h]], writes=[self.bPS[ps]])
                    ptile = self.PT[:, pt * 512:pt * 512 + 32]
                    o_ = (sq_ * 8 + kb) * 8 + h
                    tk.op(tk.act, lambda: nc.scalar.activation(out=ptile, in_=self.PS[ps][:, 0:32], func=AF.Exp, scale=SCALE, bias=btc[:, o_:o_ + 1]),
                          reads=[self.bPS[ps], bsm], writes=[self.bPT[pt]])
                    tk.op(tk.pe, lambda: nc.tensor.matmul(self.PS[5][:, qs], lhsT=self.ONESB[:], rhs=ptile, start=(not started), stop=False),
                          reads=[self.bC, self.bPT[pt]], writes=[self.bPS[5]], signal=False)
                    tk.op(tk.pe, lambda: nc.tensor.matmul(self.PS[4][:, qs], lhsT=kv[:, 1024 + kb * 128:1024 + (kb + 1) * 128], rhs=ptile,
                                                          start=(not started), stop=False), reads=[bkv, self.bPT[pt]], writes=[self.bPS[4]])
                    started = True
            ps = cnt % 2
            pt = cnt % 3
            cnt += 1
            tk.op(tk.pe, lambda: nc.tensor.matmul(self.PS[ps][0:NS, 0:NS], lhsT=kn[:, h * NS:(h + 1) * NS], rhs=self.QT[:, h * T:h * T + NS],
                                                  start=True, stop=True), reads=[bkn, self.bQT[h]], writes=[self.bPS[ps]])
            ptile = self.PT[0:NS, pt * 512:pt * 512 + NS]
            tk.op(tk.act, lambda: nc.scalar.activation(out=ptile, in_=self.PS[ps][0:NS, 0:NS], func=AF.Exp, scale=SCALE, bias=btn[:, h:h + 1]),
                  reads=[self.bPS[ps], bsm], writes=[self.bPT[pt]])
            tk.op(tk.pool, lambda: nc.gpsimd.tensor_tensor(out=ptile, in0=ptile, in1=self.M64[0:NS, :], op=ALU.mult),
                  reads=[self.bPT[pt], bm64], writes=[self.bPT[pt]])
            tk.op(tk.pe, lambda: nc.tensor.matmul(self.PS[5][:, 0:NS], lhsT=self.ONESB[0:NS, :], rhs=ptile, start=False, stop=True),
                  reads=[self.bC, self.bPT[pt]], writes=[self.bPS[5]], signal=False)
            tk.op(tk.pe, lambda: nc.tensor.matmul(self.PS[4][:, 0:NS], lhsT=vn[0:NS, h * 128:(h + 1) * 128], rhs=ptile, start=False, stop=True),
                  reads=[bvn, self.bPT[pt]], writes=[self.bPS[4]])
            tk.op(tk.dve, lambda: nc.vector.reciprocal(out=rl, in_=self.PS[5][:, 0:NS]), reads=[self.bPS[5]], writes=[brl])
            tk.op(tk.dve, lambda: nc.vector.tensor_tensor(out=self.XT[:, h * NS:(h + 1) * NS], in0=self.PS[4][:, 0:NS], in1=rl, op=ALU.mult),
                  reads=[self.bPS[4], brl], writes=[self.bXT])
        self.bDB = bdbs
        self.mix(plan["mix"], ncol=NS, db=dbs)
        self.norm_to_xt(2, nblk=1, ntok=NS, npart=NS)
        self.ffn(plan["ffn2"], ncol=NS)
        self.final_out(lambda b: self.ys_out, nblk=1, npart=NS)


    def build(self):
        nc, tk = self.nc, self.tk
        self.tmp_cur = []
        self.hm_cur = self.bHM
        self.qt_cur = self.bQT
        self.prologue()
        self.load_x(0)
        self.convert(0)
        splan = None
        plan = []
        for s in range(self.nslot):
            full = (s % 2 == 1)
            if s == min(2, self.nslot - 1) and self.with_sample:
                splan = {"ffn1": self.ffn_loads(0), "inproj": self.inproj_loads(True), "attn": self.sample_attn_loads(),
                         "mix": self.mix_loads(), "ffn2": self.ffn_loads(1)}
            ent = {"ffn1": self.ffn_loads(0), "inproj": self.inproj_loads(full)}
            if full:
                ent["attn"] = self.attn_loads(s)
                ent["mix"] = self.mix_loads()
                ent["ffn2"] = self.ffn_loads(1)
            plan.append(ent)
        import os
        stop = int(os.environ.get("KSTOP", "99"))
        for s in range(self.nslot):
            full = (s % 2 == 1)
            last_full = (s == self.nslot - 1)
            ent = plan[s]
            if stop == 0:
                break
            if s == min(2, self.nslot - 1) and self.with_sample:
                self.sample_slot(splan)
            if s > 0:
                self.load_x(s)
            self.norm_to_xt(0)
            if stop == 1:
                break
            self.ffn(ent["ffn1"])
            if s == 0:
                self.convert(1)
            if stop == 2:
                break
            self.norm_to_xt(1)
            self.inproj(s, ent["inproj"], full, last_full)
            if stop == 3:
                break
            if full:
                self.attention(s, ent["attn"])
                self.mix(ent["mix"])
                self.norm_to_xt(2)
                self.ffn(ent["ffn2"])
                i = s // 2
                self.final_out(lambda b, i=i: self.y_out[i * T + b * 128: i * T + (b + 1) * 128, :])
        for E in (tk.pe, tk.act, tk.dve, tk.pool):
            for E2 in (tk.pe, tk.act, tk.dve, tk.pool):
                if E2 is not E and E2.count > 0:
                    tk._wait(E, {E2.name: (E2.sem, E2.count)})
        tk.finish(tk.sp)
        tk.finish(tk.pool)
        return nc


_CACHE = {}


def _consts(core):
    g = core % 2
    identf = np.eye(128, dtype=np.float32)
    identb = np.eye(128).astype(ml_dtypes.bfloat16)
    k = np.arange(128)[:, None]
    q = np.arange(128)[None, :]
    trib = (k <= q).astype(np.float32).astype(ml_dtypes.bfloat16)
    onesf = np.ones((128, 128), np.float32)
    kmask = np.zeros((128, 64), np.float32)
    if g == 0:
        kmask[:, 0:8] = NEG
    diag8 = np.zeros((8, 8, 8), np.float32)
    for h in range(8):
        diag8[h, h, :] = 1.0
    poolfix = np.ones((128, 4, 16), np.float32)
    if g == 0:
        for gi, w in enumerate((2, 4, 8, 16)):
            pos = np.arange(16)
            cnt = np.minimum(pos + 1, w)
            poolfix[:, gi, :] = (w / cnt)[None, :]
    selb = np.zeros((2, 8, 8, 128), np.float32)
    for h in range(8):
        selb[:, h, h, :] = 1.0
    return dict(selb=selb.reshape(16, 1024).astype(ml_dtypes.bfloat16),
                identf=identf, identb=identb, trib=trib, onesf=onesf, kmask=kmask, diag8=diag8.reshape(8, 64),
                poolfix=poolfix.reshape(128, 64))


def kernel(x_prompt, x_sample, cache_k, cache_v, cache_logf, state_pool,
           g_ffn1, w1_gate, w1_up, w1_down, g_mix, w_in, b_f, w_pool, pool_scale,
           w_o, g_ffn2, w2_gate, w2_up, w2_down, g_final, _npair=4, _cores=8):
    npair = _npair
    ns = 2 * npair
    key = (npair,)
    if key not in _CACHE:
        _CACHE[key] = Prog(npair, True).build()
    nc = _CACHE[key]
    f32 = lambda a: np.ascontiguousarray(np.asarray(a, dtype=np.float32))
    shared = {
        "w1_gate": f32(w1_gate[0]), "w1_up": f32(w1_up[0]), "w1_down": f32(w1_down[0]),
        "w2_gate": f32(w2_gate[0]), "w2_up": f32(w2_up[0]), "w2_down": f32(w2_down[0]),
        "w_in": f32(w_in[0]),
        "w_f": f32(np.asarray(w_in[0])[:, 3072:3080].reshape(KC, 128, 8).transpose(1, 0, 2).reshape(128, KC * 8)), "w_o": f32(w_o[0]), "w_pool": f32(np.asarray(w_pool[0]).reshape(1024, 256)),
        "gcols": f32(np.concatenate([np.asarray(g).reshape(KC, 128).T for g in (g_ffn1[0], g_mix[0], g_ffn2[0])], axis=1)),
        "gfin": f32(np.broadcast_to(np.asarray(g_final)[None, :], (128, D))),
        "bfcol": f32(np.asarray(b_f[0]).reshape(8, 1)),
        "pscale": f32(np.asarray(pool_scale[0]).reshape(8, 128).T),
    }
    xp = np.asarray(x_prompt)
    in_maps = []
    for c in range(_cores):
        b, g = c // 2, c % 2
        xs = np.zeros((ns * T, D), np.float32)
        for s in range(ns):
            t = s if g == 1 else s - 1
            if t >= 0:
                xs[s * T:(s + 1) * T] = xp[b, t * T:(t + 1) * T]
        m = dict(shared)
        m["xs"] = xs
        m.update(_consts(c))
        sq = slice(2 * c, 2 * c + 2)
        m["xsmp"] = f32(np.asarray(x_sample)[sq].reshape(64, D))
        ck = np.asarray(cache_k)[0, sq]
        m["ckT"] = f32(ck.transpose(0, 2, 3, 1).reshape(2 * H * 128, 1024))
        cvv = np.asarray(cache_v)[0, sq].reshape(2, NB, 128, H, HD)
        m["cv"] = f32(cvv.transpose(0, 3, 2, 1, 4).reshape(2 * H * 128, NB * HD))
        m["clfT"] = f32(np.asarray(cache_logf)[0, sq].transpose(0, 2, 1).reshape(2 * H, 1024))
        sp = np.asarray(state_pool)[0, sq]
        spt = np.zeros((128, 2, 8, 16), np.float32)
        spt[:, :, :, 1:16] = sp.reshape(2, 15, 8, 128).transpose(3, 0, 2, 1)
        m["spT"] = spt.reshape(128, 256)
        kq = np.arange(64)
        m["mask64"] = ((kq[:, None] // 32 == kq[None, :] // 32) & (kq[:, None] <= kq[None, :])).astype(np.float32).astype(ml_dtypes.bfloat16)
        in_maps.append(m)
    res = run_bass_kernel_spmd(nc, in_maps, core_ids=list(range(_cores)))
    R = res.results
    B_ = _cores // 2
    S = ns * T if False else None
    ntile = ns
    y = np.zeros((B_, ntile * T, D), np.float32)
    kk = np.zeros((1, B_, ntile * T, H, HD), np.float32)
    vv = np.zeros((1, B_, ntile * T, H, HD), np.float32)
    lf = np.zeros((1, B_, ntile * T, H), np.float32)
    pp = np.zeros((1, B_, 15, 1024), np.float32)
    for c in range(_cores):
        b, g = c // 2, c % 2
        r = R[c]
        for i in range(npair):
            t = 2 * i + g
            y[b, t * T:(t + 1) * T] = r["y_out"][i * T:(i + 1) * T]
        if g == 1:
            kk[0, b] = r["k_out"].reshape(ntile * T, H, HD)
            vv[0, b] = r["v_out"].reshape(ntile * T, H, HD)
            lf[0, b] = r["lf_out"].reshape(ns, 128, NB, H).transpose(0, 2, 1, 3).reshape(ntile * T, H)
            pp[0, b] = r["pool_out"]
    DB = 16
    ys = np.zeros((DB, 32, D), np.float32)
    ks_ = np.zeros((1, DB, 32, H, HD), np.float32)
    vs_ = np.zeros((1, DB, 32, H, HD), np.float32)
    lfs = np.zeros((1, DB, 32, H), np.float32)
    pps = np.zeros((1, DB, 15, 1024), np.float32)
    for c in range(_cores):
        r = R[c]
        sq = slice(2 * c, 2 * c + 2)
        ys[sq] = r["ys_out"].reshape(2, 32, D)
        ks_[0, sq] = r["ks_out"].reshape(2, 32, H, HD)
        vs_[0, sq] = r["vs_out"].reshape(2, 32, H, HD)
        lfs[0, sq] = r["lfs_out"].reshape(2, 32, H)
        pps[0, sq] = r["ps_out"].reshape(2, 15, 1024)
    if _npair != 4 or _cores != 8:
        return y, kk, vv, lf, pp, ys, ks_, vs_, lfs, pps
    return y, ys, kk, vv, lf, pp, ks_, vs_, lfs, pps
```
